# Optimizing a Trainium2 kernel written in Bass

```python
import math
import jax, jax.numpy as jnp
from jax import lax
import numpy as np

D_MODEL = 1024
BATCH = 8
SEQ = 2048
DEPTH = 2

GRID_W = 64
NA_HEADS = 8
NA_HEAD_DIM = 64
NA_WIN_ROWS = 8
NA_WIN_COLS = 16
NA_QCOL_BLOCK = NA_WIN_COLS
NA_KCOL_BLOCK = 2 * NA_WIN_COLS
SW_HEADS = 8
SW_KV_HEADS = 2
SW_HEAD_DIM = 64
SW_WINDOW = 128
SW_BLOCK = 128
REL_BUCKETS = 32
REL_MAX_DIST = 128
D_FF = 2816
N_BRANCHES = 2
EPS = 1e-6
NEG = -1e30

NA_WIDTH = NA_HEADS * NA_HEAD_DIM
SW_Q_WIDTH = SW_HEADS * SW_HEAD_DIM
SW_KV_WIDTH = SW_KV_HEADS * SW_HEAD_DIM
IN_WIDTH = 3 * NA_WIDTH + SW_Q_WIDTH + 2 * SW_KV_WIDTH + N_BRANCHES * D_MODEL

kernel_name = "hybrid_natten_swa_macaron_encoder"


def rms_norm(x, g):
    xf = x.astype(jnp.float32)
    y = xf * lax.rsqrt(jnp.mean(xf * xf, axis=-1, keepdims=True) + EPS)
    return (y * g.astype(jnp.float32)).astype(x.dtype)


def swiglu(x, w_gate, w_up, w_down):
    return (jax.nn.silu(x @ w_gate) * (x @ w_up)) @ w_down


def t5_bucket(rel):
    nb = REL_BUCKETS // 2
    max_exact = nb // 2
    n = np.abs(rel)
    large = max_exact + (np.log(np.maximum(n, 1) / max_exact)
                         / np.log(REL_MAX_DIST / max_exact) * (nb - max_exact)).astype(np.int32)
    large = np.minimum(large, nb - 1)
    return ((rel > 0) * nb + np.where(n < max_exact, n, large)).astype(np.int32)


def neighbourhood_attention(q, k, v, rpb):
    B, S, H, dh = q.shape
    rows = S // GRID_W
    kr = min(NA_WIN_ROWS, rows)
    ncb = GRID_W // NA_QCOL_BLOCK
    r = np.arange(rows)
    row_start = np.clip(r - kr // 2, 0, rows - kr)
    key_rows = row_start[:, None] + np.arange(kr)
    cb = np.arange(ncb)
    kcol_start = np.clip(cb * NA_QCOL_BLOCK - NA_WIN_COLS // 2, 0, GRID_W - NA_KCOL_BLOCK)
    key_cols = kcol_start[:, None] + np.arange(NA_KCOL_BLOCK)
    key_idx = (key_rows[:, None, :, None] * GRID_W + key_cols[None, :, None, :])
    key_idx = key_idx.reshape(rows, ncb, kr * NA_KCOL_BLOCK)
    kg = k[:, key_idx]
    vg = v[:, key_idx]
    qb = q.reshape(B, rows, ncb, NA_QCOL_BLOCK, H, dh)
    s = jnp.einsum('brcqhd,brckhd->bhrcqk', qb, kg).astype(jnp.float32) / math.sqrt(dh)
    q_cols = cb[:, None] * NA_QCOL_BLOCK + np.arange(NA_QCOL_BLOCK)
    q_col_start = np.clip(q_cols - NA_WIN_COLS // 2, 0, GRID_W - NA_WIN_COLS)
    kc = key_cols[:, None, :]
    col_ok = (kc >= q_col_start[..., None]) & (kc < q_col_start[..., None] + NA_WIN_COLS)
    row_idx = key_rows - r[:, None] + NA_WIN_ROWS - 1
    col_idx = np.clip(kc - q_cols[..., None] + NA_WIN_COLS - 1, 0, 2 * NA_WIN_COLS - 2)
    bias = rpb[:, row_idx[:, None, None, :, None], col_idx[None, :, :, None, :]]
    bias = bias.reshape(H, rows, ncb, NA_QCOL_BLOCK, kr * NA_KCOL_BLOCK)
    mask = np.broadcast_to(col_ok[None, :, :, None, :],
                           (rows, ncb, NA_QCOL_BLOCK, kr, NA_KCOL_BLOCK))
    mask = mask.reshape(rows, ncb, NA_QCOL_BLOCK, kr * NA_KCOL_BLOCK)
    s = jnp.where(mask, s + bias.astype(jnp.float32), NEG)
    p = jax.nn.softmax(s, axis=-1)
    o = jnp.einsum('bhrcqk,brckhd->brcqhd', p.astype(v.dtype), vg)
    return o.reshape(B, S, H * dh)


def sliding_window_gqa(q, k, v, rel_bias, sink):
    B, S, _, dh = q.shape
    nb = S // SW_BLOCK
    rep = SW_HEADS // SW_KV_HEADS
    qb = q.reshape(B, nb, SW_BLOCK, SW_KV_HEADS, rep, dh)
    pad = ((0, 0), (SW_BLOCK, SW_BLOCK), (0, 0), (0, 0))
    kp = jnp.pad(k, pad).reshape(B, nb + 2, SW_BLOCK, SW_KV_HEADS, dh)
    vp = jnp.pad(v, pad).reshape(B, nb + 2, SW_BLOCK, SW_KV_HEADS, dh)
    kb = jnp.concatenate([kp[:, :-2], kp[:, 1:-1], kp[:, 2:]], axis=2)
    vb = jnp.concatenate([vp[:, :-2], vp[:, 1:-1], vp[:, 2:]], axis=2)
    s = jnp.einsum('bnqgrd,bnkgd->bgrnqk', qb, kb).astype(jnp.float32) / math.sqrt(dh)
    a = np.arange(SW_BLOCK)[:, None]
    j = np.arange(3 * SW_BLOCK)[None, :]
    rel = j - SW_BLOCK - a
    kpos = (np.arange(nb)[:, None, None] - 1) * SW_BLOCK + j[None]
    mask = (np.abs(rel)[None] <= SW_WINDOW) & (kpos >= 0) & (kpos < S)
    bias = rel_bias.astype(jnp.float32).reshape(SW_KV_HEADS, rep, 1, SW_BLOCK, 3 * SW_BLOCK)
    s = jnp.where(mask, s + bias, NEG)
    sink_b = sink.astype(jnp.float32).reshape(SW_KV_HEADS, rep, 1, 1, 1)
    m = jnp.maximum(s.max(axis=-1, keepdims=True), sink_b)
    e = jnp.exp(s - m)
    p = e / (e.sum(axis=-1, keepdims=True) + jnp.exp(sink_b - m))
    o = jnp.einsum('bgrnqk,bnkgd->bnqgrd', p.astype(v.dtype), vb)
    return o.reshape(B, S, SW_HEADS * dh)


def setup_inputs(seed: int = 0) -> dict:
    key = jax.random.key(seed)
    ks = jax.random.split(key, 24)
    L, D, F = DEPTH, D_MODEL, D_FF
    nrm = lambda k, shape, fan: jax.random.normal(k, shape, jnp.float32) * fan ** -0.5
    gain = lambda k, shape: 1.0 + 0.05 * jax.random.normal(k, shape, jnp.float32)
    return {
        "x": jax.random.normal(ks[0], (BATCH, SEQ, D), jnp.float32),
        "ffn1_norm": gain(ks[1], (L, D)),
        "ffn1_w_gate": nrm(ks[2], (L, D, F), D),
        "ffn1_w_up": nrm(ks[3], (L, D, F), D),
        "ffn1_w_down": nrm(ks[4], (L, F, D), F),
        "mix_norm": gain(ks[5], (L, D)),
        "w_in": nrm(ks[6], (L, D, IN_WIDTH), D),
        "b_gate": 0.01 * jax.random.normal(ks[7], (L, N_BRANCHES * D), jnp.float32),
        "na_q_norm": gain(ks[8], (L, NA_HEAD_DIM)),
        "na_k_norm": gain(ks[9], (L, NA_HEAD_DIM)),
        "na_rpb": 0.1 * jax.random.normal(ks[10], (L, NA_HEADS, 2 * NA_WIN_ROWS - 1, 2 * NA_WIN_COLS - 1), jnp.float32),
        "sw_q_norm": gain(ks[11], (L, SW_HEAD_DIM)),
        "sw_k_norm": gain(ks[12], (L, SW_HEAD_DIM)),
        "sw_sink": 0.5 * jax.random.normal(ks[13], (L, SW_HEADS), jnp.float32),
        "t5_rel_table": 0.1 * jax.random.normal(ks[14], (REL_BUCKETS, SW_HEADS), jnp.float32),
        "w_branch_na": nrm(ks[15], (L, NA_WIDTH, D), NA_WIDTH),
        "w_branch_sw": nrm(ks[16], (L, SW_Q_WIDTH, D), SW_Q_WIDTH),
        "w_out": nrm(ks[17], (L, D, D), D),
        "ffn2_norm": gain(ks[18], (L, D)),
        "ffn2_w_gate": nrm(ks[19], (L, D, F), D),
        "ffn2_w_up": nrm(ks[20], (L, D, F), D),
        "ffn2_w_down": nrm(ks[21], (L, F, D), F),
    }


def reference(x, ffn1_norm, ffn1_w_gate, ffn1_w_up, ffn1_w_down, mix_norm, w_in, b_gate,
              na_q_norm, na_k_norm, na_rpb, sw_q_norm, sw_k_norm, sw_sink, t5_rel_table,
              w_branch_na, w_branch_sw, w_out, ffn2_norm, ffn2_w_gate, ffn2_w_up, ffn2_w_down):
    B, S, D = x.shape
    rel = np.arange(3 * SW_BLOCK)[None, :] - SW_BLOCK - np.arange(SW_BLOCK)[:, None]
    t5_bias = jnp.transpose(t5_rel_table[t5_bucket(rel)], (2, 0, 1))
    splits = np.cumsum([NA_WIDTH, NA_WIDTH, NA_WIDTH, SW_Q_WIDTH, SW_KV_WIDTH, SW_KV_WIDTH])
    for l in range(DEPTH):
        x = x + 0.5 * swiglu(rms_norm(x, ffn1_norm[l]), ffn1_w_gate[l], ffn1_w_up[l], ffn1_w_down[l])
        h = rms_norm(x, mix_norm[l])
        z = h @ w_in[l]
        qa, ka, va, qs, ks_, vs, zg = jnp.split(z, splits, axis=-1)
        qa = rms_norm(qa.reshape(B, S, NA_HEADS, NA_HEAD_DIM), na_q_norm[l])
        ka = rms_norm(ka.reshape(B, S, NA_HEADS, NA_HEAD_DIM), na_k_norm[l])
        va = va.reshape(B, S, NA_HEADS, NA_HEAD_DIM)
        o_na = neighbourhood_attention(qa, ka, va, na_rpb[l])
        qs = rms_norm(qs.reshape(B, S, SW_HEADS, SW_HEAD_DIM), sw_q_norm[l])
        ks_ = rms_norm(ks_.reshape(B, S, SW_KV_HEADS, SW_HEAD_DIM), sw_k_norm[l])
        vs = vs.reshape(B, S, SW_KV_HEADS, SW_HEAD_DIM)
        o_sw = sliding_window_gqa(qs, ks_, vs, t5_bias, sw_sink[l])
        g = jax.nn.sigmoid((zg + b_gate[l]).astype(jnp.float32)).astype(x.dtype)
        g = g.reshape(B, S, N_BRANCHES, D)
        merged = g[:, :, 0] * (o_na @ w_branch_na[l]) + g[:, :, 1] * (o_sw @ w_branch_sw[l])
        x = x + merged @ w_out[l]
        x = x + 0.5 * swiglu(rms_norm(x, ffn2_norm[l]), ffn2_w_gate[l], ffn2_w_up[l], ffn2_w_down[l])
    return x
```

```python
import numpy as np
from contextlib import ExitStack
import concourse.bass as bass
import concourse.mybir as mybir
from concourse.bass_utils import run_bass_kernel_spmd

F32 = mybir.dt.float32
BF16 = mybir.dt.bfloat16
AF = mybir.ActivationFunctionType
ALU = mybir.AluOpType

NCORES = 8
L = 2
D = 1024
S_LEN = 2048
DFF = 2816
NT = 16
TT = 4
DC = 8
EPS = 1e-6
MASKV = -30000.0
FGROUPS = [(0, 8), (8, 8), (16, 6)]
NS = 10
CH_PER_LAYER = 2 * (22 * 2 + 3 * 8) + 12 + 8 + 24 + 8
NCH = L * CH_PER_LAYER

PL = 56
C_FFN1, C_MIX, C_FFN2, C_BG = 0, 8, 16, 24
C_NAQ, C_NAK, C_SWQ, C_SWK, C_SINK = 40, 41, 42, 43, 44
C_NAQ8, C_SWQ8, C_ESINK = 48, 49, 50
C_EPS = L * PL
NPARAM = L * PL + 8

ENGS = ("pe", "act", "dve", "pool", "sp")


class Buf:
    __slots__ = ("name", "w", "r")

    def __init__(self, name):
        self.name = name
        self.w = None
        self.r = {}


class Sched:
    def __init__(self, nc):
        self.nc = nc
        self.ops = {e: [] for e in ENGS}
        self.count = {e: 0 for e in ENGS}
        self.seen = {e: {} for e in ENGS}
        self.semkeys = list(ENGS)
        self.sems = {}

    def new_dma_sem(self, name):
        key = "d_" + name
        assert key not in self.count
        self.count[key] = 0
        self.semkeys.append(key)
        return key

    PARANOID = False

    def _waits(self, eng, reads, writes):
        need = {}
        if self.PARANOID:
            for k, c in self.count.items():
                if c > 0 and (k != eng):
                    isd = k.startswith("d_")
                    if self.PARANOID == "all" or (self.PARANOID == "dma" and isd) or (self.PARANOID == "eng" and not isd) \
                            or (self.PARANOID == "dmaw" and k.startswith("d_w")) or (self.PARANOID == "dmao" and isd and not k.startswith("d_w")):
                        need[k] = c
        for b in reads:
            if b.w is not None:
                k, c = b.w
                if c > need.get(k, 0):
                    need[k] = c
        for b in writes:
            if b.w is not None and b.w[0] != eng:
                k, c = b.w
                if c > need.get(k, 0):
                    need[k] = c
            for k, c in b.r.items():
                if k != eng and c > need.get(k, 0):
                    need[k] = c
        waits = []
        seen = self.seen[eng]
        for k, c in need.items():
            if seen.get(k, 0) < c:
                seen[k] = c
                waits.append((k, c))
        return waits

    def op(self, eng, fn, reads=(), writes=(), inc=True):
        waits = self._waits(eng, reads, writes)
        if inc:
            self.count[eng] += 1
            c = self.count[eng]
        else:
            c = self.count[eng] + 1
        for b in reads:
            if b.r.get(eng, 0) < c:
                b.r[eng] = c
        for b in writes:
            b.w = (eng, c)
            b.r = {}
        self.ops[eng].append((waits, fn, eng if inc else None, 1))

    def dma(self, eng, semkey, fn, reads=(), writes=()):
        waits = self._waits(eng, reads, writes)
        self.count[semkey] += 16
        c = self.count[semkey]
        for b in reads:
            if b.r.get(semkey, 0) < c:
                b.r[semkey] = c
        for b in writes:
            b.w = (semkey, c)
            b.r = {}
        self.ops[eng].append((waits, fn, semkey, 16))

    def final_wait(self, eng, bufs):
        waits = self._waits(eng, bufs, ())
        self.ops[eng].append((waits, None, None, 0))

    def emit(self, stack):
        nc = self.nc
        for k in self.semkeys:
            self.sems[k] = stack.enter_context(nc.semaphore(k))
        block = stack.enter_context(nc.Block())
        sems = self.sems

        def run(name):
            def body(e):
                for waits, fn, inck, incv in self.ops[name]:
                    for k, c in waits:
                        e.wait_ge(sems[k], c)
                    if fn is not None:
                        ins = fn(e)
                        if inck is not None:
                            ins.then_inc(sems[inck], incv)
            return body
        block.tensor(run("pe"))
        block.scalar(run("act"))
        block.vector(run("dve"))
        block.gpsimd(run("pool"))
        block.sync(run("sp"))


def t5_bucket(rel):
    nb = 16
    max_exact = 8
    n = np.abs(rel)
    large = max_exact + (np.log(np.maximum(n, 1) / max_exact) / np.log(128 / max_exact) * (nb - max_exact)).astype(np.int32)
    large = np.minimum(large, nb - 1)
    return ((rel > 0) * nb + np.where(n < max_exact, n, large)).astype(np.int32)


NA_PAIRS = [(6, 7, False), (4, 5, False), (4, 5, True), (2, 3, False), (0, 1, False), (-2, -1, False),
            (-4, -3, True), (-4, -3, False), (-6, -5, False)]


def na_index_tables():
    kap = np.repeat(np.arange(2), 64)[:, None]
    kc = np.tile(np.arange(64), 2)[:, None]
    qc = np.arange(64)[None, :]
    ridx = np.zeros((128, 9 * 128), np.int64)
    cidx = np.zeros((128, 9 * 128), np.int64)
    valid = np.zeros((128, 9 * 128), bool)
    qcs = np.clip(qc - 8, 0, 48)
    col_ok = (kc >= qcs) & (kc < qcs + 16)
    col_i = np.clip(kc - qc + 15, 0, 30)
    for a, (r0, r1, interior) in enumerate(NA_PAIRS):
        for par, rho in enumerate((r0, r1)):
            dr = kap - rho
            rv = (dr >= -7) & (dr <= 7)
            if interior:
                rv = rv & (dr >= -4) & (dr <= 3)
            sl = slice(a * 128 + par * 64, a * 128 + par * 64 + 64)
            ridx[:, sl] = np.broadcast_to(np.clip(dr + 7, 0, 14), (128, 64))
            cidx[:, sl] = col_i
            valid[:, sl] = rv & col_ok
    return ridx, cidx, valid


def na_tile_info(i):
    if 2 <= i <= 13:
        return i - 2, i + 2, [(0, 5, 2)]
    if i == 0:
        return 0, 3, [(0, 2, 4), (2, 2, 7)]
    if i == 1:
        return 0, 3, [(0, 3, 3), (3, 1, 7)]
    if i == 14:
        return 12, 15, [(0, 1, 1), (1, 3, 3)]
    return 12, 15, [(0, 2, 0), (2, 2, 3)]


def colchunk(W, c0, ncol=128):
    return W[:, c0:c0 + ncol].reshape(8, 128, ncol).transpose(1, 0, 2)


def build_wstream(inp):
    ws = np.zeros((NCH, 128, 8, 128), np.float32)
    n = 0

    def ffn(wg, wu, wd):
        nonlocal n
        for f0, nf in FGROUPS:
            for f in range(f0, f0 + nf):
                ws[n] = colchunk(wg, f * 128); n += 1
                ws[n] = colchunk(wu, f * 128); n += 1
            for dc in range(8):
                blk = wd[f0 * 128:(f0 + nf) * 128, dc * 128:(dc + 1) * 128].reshape(nf, 128, 128).transpose(1, 0, 2)
                ws[n, :, :nf, :] = blk; n += 1

    for l in range(L):
        ffn(inp["ffn1_w_gate"][l], inp["ffn1_w_up"][l], inp["ffn1_w_down"][l])
        win = inp["w_in"][l]
        for p in range(4):
            ws[n] = colchunk(win, p * 128); n += 1
            ws[n] = colchunk(win, 512 + p * 128); n += 1
            ws[n] = colchunk(win, 1024 + p * 128); n += 1
        for g in range(2):
            ws[n] = colchunk(win, 1536 + (2 * g) * 128); n += 1
            ws[n] = colchunk(win, 1536 + (2 * g + 1) * 128); n += 1
            kd = colchunk(win, 2048 + g * 64, 64)
            ws[n, :, :, 0:64] = kd; ws[n, :, :, 64:128] = kd; n += 1
            vd = colchunk(win, 2176 + g * 64, 64)
            ws[n, :, :, 0:64] = vd; ws[n, :, :, 64:128] = vd; n += 1
        wbr = np.concatenate([inp["w_branch_na"][l], inp["w_branch_sw"][l]], axis=0)
        for dc in range(8):
            ws[n] = colchunk(win, 2304 + dc * 128); n += 1
            ws[n] = colchunk(win, 2304 + 1024 + dc * 128); n += 1
            ws[n] = colchunk(wbr, dc * 128); n += 1
        for dc in range(8):
            ws[n] = colchunk(inp["w_out"][l], dc * 128); n += 1
        ffn(inp["ffn2_w_gate"][l], inp["ffn2_w_up"][l], inp["ffn2_w_down"][l])
    assert n == NCH
    return ws.reshape(NCH, 128, 1024)


def build_params(inp):
    P = np.zeros((128, NPARAM), np.float32)
    for l in range(L):
        o = l * PL
        P[:, o + C_FFN1:o + C_FFN1 + 8] = inp["ffn1_norm"][l].reshape(8, 128).T
        P[:, o + C_MIX:o + C_MIX + 8] = inp["mix_norm"][l].reshape(8, 128).T
        P[:, o + C_FFN2:o + C_FFN2 + 8] = inp["ffn2_norm"][l].reshape(8, 128).T
        P[:, o + C_BG:o + C_BG + 16] = inp["b_gate"][l].reshape(16, 128).T
        P[:, o + C_NAQ] = np.tile(inp["na_q_norm"][l], 2)
        P[:, o + C_NAK] = np.tile(inp["na_k_norm"][l], 2)
        P[:, o + C_SWQ] = np.tile(inp["sw_q_norm"][l], 2)
        P[:, o + C_SWK] = np.tile(inp["sw_k_norm"][l], 2)
        for pp in range(4):
            P[:64, o + C_SINK + pp] = inp["sw_sink"][l][2 * pp]
            P[64:, o + C_SINK + pp] = inp["sw_sink"][l][2 * pp + 1]
    P[:, C_EPS] = EPS
    return P


def build_consts():
    C = np.zeros((128, 384), np.float32)
    C[:, 0:128] = np.eye(128, dtype=np.float32)
    C[:, 128:256] = 1.0 / 1024.0
    C[0:64, 256:320] = 1.0 / 64.0
    C[64:128, 320:384] = 1.0 / 64.0
    return C


def build_bias_tables(inp):
    kp = np.arange(128)[:, None, None]
    c = np.arange(3)[None, :, None]
    qa = np.arange(128)[None, None, :]
    rel = (c - 1) * 128 + kp - qa
    tb = np.asarray(inp["t5_rel_table"])[t5_bucket(rel)]
    swb = np.ascontiguousarray(tb.transpose(3, 0, 1, 2)).reshape(2, 4, 128, 384).transpose(0, 2, 1, 3).reshape(2, 128, 4 * 384)
    swm = np.where(np.abs(rel) <= 128, 0.0, MASKV).astype(np.float32).reshape(128, 384)
    ridx, cidx, valid = na_index_tables()
    rpb = np.asarray(inp["na_rpb"])
    g = rpb[:, :, ridx, cidx]
    nab = np.ascontiguousarray(g.reshape(L, 4, 2, 128, 1152).transpose(0, 1, 3, 2, 4)).reshape(L, 4, 128, 2304)
    nam = np.where(valid, 0.0, MASKV).astype(np.float32)
    return np.ascontiguousarray(swb, dtype=np.float32), swm, np.ascontiguousarray(nab, dtype=np.float32), nam


P_XT = 0
P_HT = P_XT + 65536
P_RING = P_HT + 32768
P_CONST = P_RING + NS * 2048
P_PARAM = P_CONST + 1536 + 256
P_NAM = P_PARAM + NPARAM * 4
P_SWM = P_NAM + 1152 * 4
P_TMP = P_SWM + 384 * 4
P_PH = P_TMP + 6 * 2048
ARENA_BYTES = P_PH + 65536
assert ARENA_BYTES <= 212800, ARENA_BYTES
PH_ACT = 0
PH_STMP = 32768
PH_STAGE = 49152
PH_O = 0
PH_QA = 32768
PH_QB = PH_QA + 4096
PH_K = PH_QB + 4096
PH_V = PH_K + 4096
PH_TB = PH_V + 4096
PH_PT = PH_TB + 9216
PH_RD = PH_PT + 5120
PH_MRG = 32768
assert PH_RD + 2048 <= 65536


def build_nc(stop=None, dump=None):
    nc = bass.Bass("TRN2", target_bir_lowering=False)
    x_d = nc.dram_tensor("x", [S_LEN, D], F32, kind="ExternalInput").ap()
    ws_d = nc.dram_tensor("wstream", [NCH, 128, 1024], F32, kind="ExternalInput").ap()
    par_d = nc.dram_tensor("params", [128, NPARAM], F32, kind="ExternalInput").ap()
    con_d = nc.dram_tensor("consts", [128, 384], F32, kind="ExternalInput").ap()
    swb_d = nc.dram_tensor("swb", [2, 128, 1536], F32, kind="ExternalInput").ap()
    swm_d = nc.dram_tensor("swm", [128, 384], F32, kind="ExternalInput").ap()
    nab_d = nc.dram_tensor("nab", [L, 4, 128, 2304], F32, kind="ExternalInput").ap()
    nam_d = nc.dram_tensor("nam", [128, 1152], F32, kind="ExternalInput").ap()
    y_d = nc.dram_tensor("y", [S_LEN, D], F32, kind="ExternalOutput").ap()

    with ExitStack() as st:
        S = Sched(nc)
        arena = st.enter_context(nc.sbuf_tensor("arena", [128, ARENA_BYTES // 4], F32))
        ps = st.enter_context(nc.psum_tensor("ps", [128, 4096], F32))

        def f32v(off, n):
            return arena[:, off // 4: off // 4 + n]

        def bf16v(off, n):
            return arena[:, off // 4: off // 4 + n // 2].bitcast(BF16)

        bank = [ps[:, b * 512:(b + 1) * 512] for b in range(8)]
        pb = [Buf("bank%d" % b) for b in range(8)]

        xT = f32v(P_XT, 8 * 2048).rearrange("p (c t) -> p c t", c=8)
        hT = bf16v(P_HT, 8 * 2048).rearrange("p (c t) -> p c t", c=8)
        xbuf = [[Buf("x%d_%d" % (dc, tt)) for tt in range(TT)] for dc in range(DC)]
        hbuf = [Buf("h%d" % tt) for tt in range(TT)]
        slot = [bf16v(P_RING + s * 2048, 1024).rearrange("p (k j) -> p k j", k=8) for s in range(NS)]
        slot2 = [bf16v(P_RING + s * 2048, 1024) for s in range(NS)]
        slotbuf = [Buf("slot%d" % s) for s in range(NS)]
        wsem = [S.new_dma_sem("w%d" % s) for s in range(NS)]
        ident = f32v(P_CONST, 128)
        ones1k = f32v(P_CONST + 512, 128)
        blk64 = f32v(P_CONST + 1024, 128)
        onesb = bf16v(P_CONST + 1536, 128)
        constbuf = Buf("const")
        onesbbuf = Buf("onesb")
        param = f32v(P_PARAM, NPARAM)
        parambuf = Buf("param")
        nam = f32v(P_NAM, 1152)
        swm = f32v(P_SWM, 384)
        maskbuf = Buf("mask")
        tslot = [f32v(P_TMP + k * 2048, 512) for k in range(6)]
        tbuf = [Buf("tslot%d" % k) for k in range(6)]
        PH = P_PH

        def pcol(c):
            return param[:, c:c + 1]

        ring = {"load": 0, "use": 0, "rel": 0, "done": set()}

        def issue_load():
            i = ring["load"]
            if i >= NCH:
                return
            s = i % NS
            S.dma("pool", wsem[s], lambda e, i=i, s=s: e.dma_start(out=slot2[s], in_=ws_d[i]), writes=[slotbuf[s]])
            ring["load"] += 1

        def chunk():
            i = ring["use"]
            ring["use"] += 1
            assert i < ring["load"], "weight ring underflow"
            s = i % NS
            return slot[s], slotbuf[s], i

        def release(i):
            ring["done"].add(i)
            while ring["rel"] in ring["done"]:
                ring["done"].remove(ring["rel"])
                ring["rel"] += 1
                issue_load()

        def mm(out, lhsT, rhs, start, stop, reads, writes, inc):
            S.op("pe", lambda e: e.matmul(out, lhsT, rhs, start=start, stop=stop), reads=reads, writes=writes, inc=inc)

        csem = [S.new_dma_sem("c%d" % k) for k in range(4)]
        S.dma("sp", csem[0], lambda e: e.dma_start(out=f32v(P_CONST, 384), in_=con_d[:, :]), writes=[constbuf])
        S.dma("sp", csem[1], lambda e: e.dma_start(out=param, in_=par_d[:, :]), writes=[parambuf])
        S.dma("sp", csem[2], lambda e: e.dma_start(out=nam, in_=nam_d[:, :]), writes=[maskbuf])
        S.dma("sp", csem[3], lambda e: e.dma_start(out=swm, in_=swm_d[:, :]), writes=[maskbuf])
        S.op("dve", lambda e: e.memset(onesb, 1.0), writes=[onesbbuf])
        for i in range(NS):
            issue_load()
        for l in range(L):
            o = l * PL
            S.op("dve", lambda e, o=o: e.tensor_scalar(param[:, o + C_NAQ8:o + C_NAQ8 + 1], param[:, o + C_NAQ:o + C_NAQ + 1], 0.125, None, op0=ALU.mult),
                 reads=[parambuf], writes=[parambuf])
            S.op("dve", lambda e, o=o: e.tensor_scalar(param[:, o + C_SWQ8:o + C_SWQ8 + 1], param[:, o + C_SWQ:o + C_SWQ + 1], 0.125, None, op0=ALU.mult),
                 reads=[parambuf], writes=[parambuf])
            S.op("act", lambda e, o=o: e.activation(param[:, o + C_ESINK:o + C_ESINK + 4], param[:, o + C_SINK:o + C_SINK + 4], AF.Exp),
                 reads=[parambuf], writes=[parambuf])

        stage = [f32v(PH + PH_STAGE + s * 4096, 1024) for s in range(4)]
        stagebuf = [Buf("stage%d" % s) for s in range(4)]
        xsem = [S.new_dma_sem("x%d" % s) for s in range(4)]
        ysem = [S.new_dma_sem("y%d" % s) for s in range(4)]
        tu = [0]
        for i in range(NT):
            s = i % 4
            S.dma("sp", xsem[s], lambda e, i=i, s=s: e.dma_start(out=stage[s], in_=x_d[i * 128:(i + 1) * 128, :]), writes=[stagebuf[s]])
            for half in range(2):
                bk = tu[0] % 2
                tu[0] += 1
                for q in range(4):
                    dc = half * 4 + q
                    S.op("pe", lambda e, s=s, dc=dc, bk=bk, q=q: e.transpose(bank[bk][:, q * 128:(q + 1) * 128], stage[s][:, dc * 128:(dc + 1) * 128], ident),
                         reads=[stagebuf[s], constbuf], writes=[pb[bk]], inc=(q == 3))
                S.op("dve", lambda e, i=i, half=half, bk=bk: e.tensor_copy(xT[:, half * 4:half * 4 + 4, i * 128:(i + 1) * 128],
                                                                          bank[bk].rearrange("p (c t) -> p c t", c=4)),
                     writes=[pb[bk]] + [xbuf[half * 4 + q][i // 4] for q in range(4)])

        def tts(tt):
            return slice(tt * 512, (tt + 1) * 512)

        def rmsnorm(gcol0):
            for tt in range(TT):
                pst = 6 + (tt % 2)
                for dc in range(DC):
                    k = dc % 2
                    S.op("act", lambda e, dc=dc, tt=tt, k=k: e.activation(tslot[k], xT[:, dc, tts(tt)], AF.Square),
                         reads=[xbuf[dc][tt]], writes=[tbuf[k]])
                    mm(bank[pst], ones1k, tslot[k], dc == 0, dc == DC - 1, [tbuf[k], constbuf], [pb[pst]], True)
                k2 = 2 + tt % 2
                k4 = 4 + tt % 2
                S.op("act", lambda e, pst=pst, k2=k2: e.activation(tslot[k2], bank[pst], AF.Ln, bias=pcol(C_EPS)),
                     reads=[parambuf], writes=[pb[pst], tbuf[k2]])
                S.op("act", lambda e, k2=k2, k4=k4: e.activation(tslot[k4], tslot[k2], AF.Exp, scale=-0.5), reads=[tbuf[k2]], writes=[tbuf[k4]])
                for dc in range(DC):
                    S.op("dve", lambda e, dc=dc, tt=tt, k4=k4: e.scalar_tensor_tensor(hT[:, dc, tts(tt)], xT[:, dc, tts(tt)], pcol(gcol0 + dc), tslot[k4],
                                                                                     op0=ALU.mult, op1=ALU.mult),
                         reads=[xbuf[dc][tt], tbuf[k4], parambuf], writes=[hbuf[tt]])

        un = {"g": 0, "d": 0, "q": 0, "v": 0, "m": 0, "w": 0}

        def ffn():
            act = bf16v(PH + PH_ACT, 8 * 2048).rearrange("p (f t) -> p f t", f=8)
            stmp = [f32v(PH + PH_STMP + k * 2048, 512) for k in range(2)]
            for f0, nf in FGROUPS:
                for fi in range(nf):
                    G, Gb, Gi = chunk()
                    U, Ub, Ui = chunk()
                    for tt in range(TT):
                        u = un["g"] % 2
                        un["g"] += 1
                        bg, bu = u, 2 + u
                        for kc in range(DC):
                            mm(bank[bg], G[:, kc, :], hT[:, kc, tts(tt)], kc == 0, kc == DC - 1, [Gb, hbuf[tt]], [pb[bg]], kc == DC - 1)
                        for kc in range(DC):
                            mm(bank[bu], U[:, kc, :], hT[:, kc, tts(tt)], kc == 0, kc == DC - 1, [Ub, hbuf[tt]], [pb[bu]], kc == DC - 1)
                        S.op("act", lambda e, bg=bg, u=u: e.activation(stmp[u], bank[bg], AF.Silu), writes=[pb[bg], sbuf_[u]])
                        S.op("dve", lambda e, bu=bu, u=u, fi=fi, tt=tt: e.tensor_tensor(act[:, fi, tts(tt)], stmp[u], bank[bu], op=ALU.mult),
                             reads=[sbuf_[u]], writes=[pb[bu], actbuf[fi][tt]])
                    release(Gi)
                    release(Ui)
                for dc in range(DC):
                    Dk, Db, Di = chunk()
                    for tt in range(TT):
                        bd = 4 + un["d"] % 2
                        un["d"] += 1
                        for fi in range(nf):
                            mm(bank[bd], Dk[:, fi, :], act[:, fi, tts(tt)], fi == 0, fi == nf - 1, [Db, actbuf[fi][tt]], [pb[bd]], fi == nf - 1)
                        S.op("dve", lambda e, bd=bd, dc=dc, tt=tt: e.scalar_tensor_tensor(xT[:, dc, tts(tt)], bank[bd], 0.5, xT[:, dc, tts(tt)],
                                                                                         op0=ALU.mult, op1=ALU.add),
                             reads=[xbuf[dc][tt]], writes=[pb[bd], xbuf[dc][tt]])
                    release(Di)

        sbuf_ = [Buf("stmp%d" % k) for k in range(2)]
        actbuf = [[Buf("act%d_%d" % (fi, tt)) for tt in range(TT)] for fi in range(8)]

        oT = [bf16v(PH + PH_O + c * 4096, 2048) for c in range(8)]
        obuf = [[Buf("o%d_%d" % (c, tt)) for tt in range(TT)] for c in range(8)]
        qA = bf16v(PH + PH_QA, 2048)
        qB = bf16v(PH + PH_QB, 2048)
        kk = bf16v(PH + PH_K, 2048)
        vtm128 = bf16v(PH + PH_V, 2048).rearrange("p (t f) -> p t f", t=16)
        vtm64 = bf16v(PH + PH_V, 1024).rearrange("p (t f) -> p t f", t=16)
        qAbuf = [Buf("qA%d" % tt) for tt in range(TT)]
        qBbuf = [Buf("qB%d" % tt) for tt in range(TT)]
        kbuf = [Buf("k%d" % tt) for tt in range(TT)]
        vbuf = [Buf("v%d" % tt) for tt in range(TT)]
        tb = f32v(PH + PH_TB, 2304)
        tbbuf = Buf("tb")
        tbsem = S.new_dma_sem("tb")
        pT = [bf16v(PH + PH_PT + k * 1280, 640) for k in range(4)]
        pTbuf = [Buf("pT%d" % k) for k in range(4)]
        rden = [f32v(PH + PH_RD + k * 512, 128) for k in range(4)]
        rdbuf = [Buf("rd%d" % k) for k in range(4)]
        merged = bf16v(PH + PH_MRG, 8 * 2048).rearrange("p (c t) -> p c t", c=8)
        mbuf = [[Buf("m%d_%d" % (dc, tt)) for tt in range(TT)] for dc in range(DC)]

        def project_qk(W, Wb, dst, dstbuf, gcol):
            for tt in range(TT):
                u = un["q"] % 2
                un["q"] += 1
                bq, bm = u, 2 + u
                for kc in range(DC):
                    mm(bank[bq], W[:, kc, :], hT[:, kc, tts(tt)], kc == 0, kc == DC - 1, [Wb, hbuf[tt]], [pb[bq]], kc == DC - 1)
                S.op("act", lambda e, bq=bq, u=u: e.activation(tslot[u], bank[bq], AF.Square), writes=[pb[bq], tbuf[u]])
                mm(bank[bm], blk64, tslot[u], True, True, [tbuf[u], constbuf], [pb[bm]], True)
                S.op("act", lambda e, bm=bm, u=u: e.activation(tslot[2 + u], bank[bm], AF.Ln, bias=pcol(C_EPS)),
                     reads=[parambuf], writes=[pb[bm], tbuf[2 + u]])
                S.op("act", lambda e, u=u: e.activation(tslot[4 + u], tslot[2 + u], AF.Exp, scale=-0.5), reads=[tbuf[2 + u]], writes=[tbuf[4 + u]])
                S.op("dve", lambda e, bq=bq, u=u, tt=tt: e.scalar_tensor_tensor(dst[:, tts(tt)], bank[bq], pcol(gcol), tslot[4 + u], op0=ALU.mult, op1=ALU.mult),
                     reads=[tbuf[4 + u], parambuf], writes=[pb[bq], dstbuf[tt]])

        def project_v(W, Wb, nfeat):
            vt = vtm128 if nfeat == 128 else vtm64
            tpb = 512 // nfeat
            for g0 in range(0, NT, tpb):
                bv = 4 + un["v"] % 2
                un["v"] += 1
                for ti in range(tpb):
                    tile = g0 + ti
                    for kc in range(DC):
                        mm(bank[bv][:, ti * nfeat:(ti + 1) * nfeat], hT[:, kc, tile * 128:(tile + 1) * 128], W[:, kc, 0:nfeat],
                           kc == 0, kc == DC - 1, [Wb, hbuf[tile // 4]], [pb[bv]], kc == DC - 1 and ti == tpb - 1)
                S.op("act", lambda e, bv=bv, g0=g0: e.activation(vt[:, g0:g0 + tpb, :], bank[bv].rearrange("p (t f) -> p t f", t=tpb), AF.Copy),
                     writes=[pb[bv]] + sorted({vbuf[(g0 + ti) // 4] for ti in range(tpb)}, key=lambda b: b.name))

        def attn_pair(q, qbuf, kview, vfn, tiles, bias_off, sink_col, oc, sdouble=False):
            state = {}

            def qk_stage(i):
                jlo, jhi, segs = tiles(i)
                nk = jhi - jlo + 1
                def sbase(hh):
                    return (2 * hh + (i % 2)) * 512 if sdouble else hh * 1024

                def sbanks(hh):
                    if sdouble:
                        return [pb[2 * hh + (i % 2)]]
                    return [pb[2 * hh]] + ([pb[2 * hh + 1]] if nk > 4 else [])
                for kt in range(nk):
                    j = jlo + kt
                    for hh in range(2):
                        hs = slice(hh * 64, (hh + 1) * 64)
                        base = sbase(hh)
                        bk = (base + kt * 128) // 512
                        mm(ps[:, base + kt * 128: base + (kt + 1) * 128], kview[hs, j * 128:(j + 1) * 128], q[hs, i * 128:(i + 1) * 128],
                           True, True, [kbuf[j // 4], qbuf[i // 4]], [pb[bk]], kt == nk - 1 and hh == 1)
                for hh in range(2):
                    base = sbase(hh)
                    pbs = sbanks(hh)
                    for (c0, n, a0) in segs:
                        S.op("dve", lambda e, base=base, c0=c0, n=n, a0=a0, hh=hh: e.tensor_tensor(
                            ps[:, base + c0 * 128: base + (c0 + n) * 128], ps[:, base + c0 * 128: base + (c0 + n) * 128],
                            tb[:, bias_off(hh) + a0 * 128: bias_off(hh) + (a0 + n) * 128], op=ALU.add),
                            reads=[tbbuf], writes=pbs)
                    pi = un["w"] % 4
                    un["w"] += 1
                    S.op("act", lambda e, base=base, nk=nk, pi=pi: e.activation(pT[pi][:, 0:nk * 128], ps[:, base: base + nk * 128], AF.Exp),
                         writes=pbs + [pTbuf[pi]])
                    state[(i, hh)] = pi

            def pv_stage(i):
                jlo, jhi, segs = tiles(i)
                nk = jhi - jlo + 1
                bo = 6 + i % 2
                pis = [state.pop((i, hh)) for hh in range(2)]
                for kt in range(nk):
                    j = jlo + kt
                    for hh in range(2):
                        hs = slice(hh * 64, (hh + 1) * 64)
                        mm(bank[bo][hs, 0:128], vfn(j, hh), pT[pis[hh]][:, kt * 128:(kt + 1) * 128], kt == 0, kt == nk - 1,
                           [vbuf[j // 4], pTbuf[pis[hh]]], [pb[bo]], False)
                for kt in range(nk):
                    for hh in range(2):
                        hs = slice(hh * 64, (hh + 1) * 64)
                        mm(bank[bo][hs, 128:256], onesb[:, 0:64], pT[pis[hh]][:, kt * 128:(kt + 1) * 128], kt == 0, kt == nk - 1,
                           [onesbbuf, pTbuf[pis[hh]]], [pb[bo]], kt == nk - 1 and hh == 1)

            def norm_stage(i):
                bo = 6 + i % 2
                ri = i % 4
                if sink_col is not None:
                    S.op("act", lambda e, bo=bo, ri=ri: e.activation(rden[ri], bank[bo][:, 128:256], AF.Ln, bias=pcol(sink_col)),
                         reads=[parambuf], writes=[pb[bo], rdbuf[ri]])
                else:
                    S.op("act", lambda e, bo=bo, ri=ri: e.activation(rden[ri], bank[bo][:, 128:256], AF.Ln), writes=[pb[bo], rdbuf[ri]])
                S.op("act", lambda e, ri=ri: e.activation(rden[ri], rden[ri], AF.Exp, scale=-1.0), reads=[rdbuf[ri]], writes=[rdbuf[ri]])
                S.op("dve", lambda e, bo=bo, ri=ri, i=i: e.tensor_tensor(oT[oc][:, i * 128:(i + 1) * 128], bank[bo][:, 0:128], rden[ri], op=ALU.mult),
                     reads=[rdbuf[ri]], writes=[pb[bo], obuf[oc][i // 4]])

            for i in range(NT + 2):
                if i < NT:
                    qk_stage(i)
                if 1 <= i <= NT:
                    pv_stage(i - 1)
                if i >= 2:
                    norm_stage(i - 2)

        def sw_tiles(n):
            lo, hi = max(n - 1, 0), min(n + 1, NT - 1)
            return lo, hi, [(0, hi - lo + 1, lo - n + 1)]

        tb_alias = [stagebuf[0], stagebuf[1], stagebuf[2]] + [mbuf[dc][tt] for dc in (4, 5, 6) for tt in range(TT)]

        def mixer(l):
            o = l * PL
            for p in range(4):
                Q, Qb, Qi = chunk()
                K, Kb, Ki = chunk()
                V, Vb, Vi = chunk()
                project_qk(Q, Qb, qA, qAbuf, o + C_NAQ8)
                release(Qi)
                project_qk(K, Kb, kk, kbuf, o + C_NAK)
                release(Ki)
                project_v(V, Vb, 128)
                release(Vi)
                if stop == (l, "proj"):
                    for tt in range(TT):
                        S.op("dve", lambda e, tt=tt: e.tensor_copy(xT[:, 0, tts(tt)], qA[:, tts(tt)]), reads=[qAbuf[tt]], writes=[xbuf[0][tt]])
                        S.op("dve", lambda e, tt=tt: e.tensor_copy(xT[:, 1, tts(tt)], kk[:, tts(tt)]), reads=[kbuf[tt]], writes=[xbuf[1][tt]])
                    return
                S.dma("sp", tbsem, lambda e, l=l, p=p: e.dma_start(out=tb[:, 0:2304], in_=nab_d[l, p]), writes=[tbbuf] + tb_alias)
                for hh in range(2):
                    S.op("dve", lambda e, hh=hh: e.tensor_tensor(tb[:, hh * 1152:(hh + 1) * 1152], tb[:, hh * 1152:(hh + 1) * 1152], nam, op=ALU.add),
                         reads=[tbbuf, maskbuf], writes=[tbbuf])
                attn_pair(qA, qAbuf, kk, lambda j, hh: vtm128[:, j, hh * 64:(hh + 1) * 64], na_tile_info,
                          lambda hh: hh * 1152, None, p)
            for g in range(2):
                Q0, Q0b, Q0i = chunk()
                Q1, Q1b, Q1i = chunk()
                K, Kb, Ki = chunk()
                V, Vb, Vi = chunk()
                project_qk(Q0, Q0b, qA, qAbuf, o + C_SWQ8)
                release(Q0i)
                project_qk(Q1, Q1b, qB, qBbuf, o + C_SWQ8)
                release(Q1i)
                project_qk(K, Kb, kk, kbuf, o + C_SWK)
                release(Ki)
                project_v(V, Vb, 64)
                release(Vi)
                S.dma("sp", tbsem, lambda e, g=g: e.dma_start(out=tb[:, 0:1536], in_=swb_d[g]), writes=[tbbuf] + tb_alias)
                for h4 in range(4):
                    S.op("dve", lambda e, h4=h4: e.tensor_tensor(tb[:, h4 * 384:(h4 + 1) * 384], tb[:, h4 * 384:(h4 + 1) * 384], swm, op=ALU.add),
                         reads=[tbbuf, maskbuf], writes=[tbbuf])
                for pp in range(2):
                    attn_pair(qA if pp == 0 else qB, qAbuf if pp == 0 else qBbuf, kk, lambda j, hh: vtm64[:, j, :], sw_tiles,
                              lambda hh, pp=pp: (2 * pp + hh) * 384, o + C_ESINK + 2 * g + pp, 4 + 2 * g + pp, sdouble=True)
            for dc in range(DC):
                GA, GAb, GAi = chunk()
                GB, GBb, GBi = chunk()
                BR, BRb, BRi = chunk()
                for tt in range(TT):
                    u = un["m"] % 2
                    un["m"] += 1
                    b0 = 4 * u
                    for kc in range(DC):
                        mm(bank[b0], GA[:, kc, :], hT[:, kc, tts(tt)], kc == 0, kc == DC - 1, [GAb, hbuf[tt]], [pb[b0]], kc == DC - 1)
                    for kc in range(DC):
                        mm(bank[b0 + 1], GB[:, kc, :], hT[:, kc, tts(tt)], kc == 0, kc == DC - 1, [GBb, hbuf[tt]], [pb[b0 + 1]], kc == DC - 1)
                    for c in range(4):
                        mm(bank[b0 + 2], BR[:, c, :], oT[c][:, tts(tt)], c == 0, c == 3, [BRb, obuf[c][tt]], [pb[b0 + 2]], c == 3)
                    for c in range(4):
                        mm(bank[b0 + 3], BR[:, 4 + c, :], oT[4 + c][:, tts(tt)], c == 0, c == 3, [BRb, obuf[4 + c][tt]], [pb[b0 + 3]], c == 3)
                    tA, tB = tslot[u], tslot[2 + u]
                    S.op("act", lambda e, b0=b0, tA=tA, dc=dc: e.activation(tA, bank[b0], AF.Sigmoid, bias=pcol(o + C_BG + dc)),
                         reads=[parambuf], writes=[pb[b0], tbuf[u]])
                    S.op("act", lambda e, b0=b0, tB=tB, dc=dc: e.activation(tB, bank[b0 + 1], AF.Sigmoid, bias=pcol(o + C_BG + 8 + dc)),
                         reads=[parambuf], writes=[pb[b0 + 1], tbuf[2 + u]])
                    S.op("dve", lambda e, b0=b0, tA=tA: e.tensor_tensor(tA, tA, bank[b0 + 2], op=ALU.mult), reads=[tbuf[u]], writes=[pb[b0 + 2], tbuf[u]])
                    S.op("dve", lambda e, b0=b0, tB=tB: e.tensor_tensor(tB, tB, bank[b0 + 3], op=ALU.mult), reads=[tbuf[2 + u]], writes=[pb[b0 + 3], tbuf[2 + u]])
                    S.op("dve", lambda e, tA=tA, tB=tB, dc=dc, tt=tt: e.tensor_tensor(merged[:, dc, tts(tt)], tA, tB, op=ALU.add),
                         reads=[tbuf[u], tbuf[2 + u]], writes=[mbuf[dc][tt]])
                release(GAi)
                release(GBi)
                release(BRi)
            for dcp in range(DC):
                WO, WOb, WOi = chunk()
                for tt in range(TT):
                    bw = un["d"] % 2 + 4
                    un["d"] += 1
                    for dc in range(DC):
                        mm(bank[bw], WO[:, dc, :], merged[:, dc, tts(tt)], dc == 0, dc == DC - 1, [WOb, mbuf[dc][tt]], [pb[bw]], dc == DC - 1)
                    S.op("dve", lambda e, bw=bw, dcp=dcp, tt=tt: e.tensor_tensor(xT[:, dcp, tts(tt)], bank[bw], xT[:, dcp, tts(tt)], op=ALU.add),
                         reads=[xbuf[dcp][tt]], writes=[pb[bw], xbuf[dcp][tt]])
                release(WOi)

        def run():
            if stop == (0, "load"):
                return
            for l in range(L):
                o = l * PL
                rmsnorm(o + C_FFN1)
                ffn()
                if stop == (l, "ffn1"):
                    return
                rmsnorm(o + C_MIX)
                if stop == (l, "norm"):
                    return
                mixer(l)
                if stop == (l, "proj"):
                    return
                if stop == (l, "mixer"):
                    return
                rmsnorm(o + C_FFN2)
                ffn()
                if stop == (l, "ffn2"):
                    return
        run()

        if dump == "h":
            for dc in range(DC):
                for tt in range(TT):
                    S.op("dve", lambda e, dc=dc, tt=tt: e.tensor_copy(xT[:, dc, tts(tt)], hT[:, dc, tts(tt)]), reads=[hbuf[tt]], writes=[xbuf[dc][tt]])
        if dump == "o":
            for c in range(8):
                for tt in range(TT):
                    S.op("dve", lambda e, c=c, tt=tt: e.tensor_copy(xT[:, c, tts(tt)], oT[c][:, tts(tt)]), reads=[obuf[c][tt]], writes=[xbuf[c][tt]])

        ybufs = []
        for i in range(NT):
            s = i % 4
            for half in range(2):
                bk = tu[0] % 2
                tu[0] += 1
                for q in range(4):
                    dc = half * 4 + q
                    S.op("pe", lambda e, i=i, dc=dc, bk=bk, q=q: e.transpose(bank[bk][:, q * 128:(q + 1) * 128], xT[:, dc, i * 128:(i + 1) * 128], ident),
                         reads=[xbuf[dc][i // 4], constbuf], writes=[pb[bk]], inc=(q == 3))
                S.op("dve", lambda e, s=s, half=half, bk=bk: e.tensor_copy(stage[s][:, half * 512:(half + 1) * 512], bank[bk]),
                     writes=[pb[bk], stagebuf[s]])
            yb = Buf("y%d" % i)
            ybufs.append(yb)
            S.dma("sp", ysem[s], lambda e, i=i, s=s: e.dma_start(out=y_d[i * 128:(i + 1) * 128, :], in_=stage[s]), reads=[stagebuf[s]], writes=[yb])
        S.final_wait("sp", ybufs)
        S.emit(st)
    return nc


_HOST_CACHE = {}


def prep_shared(inputs):
    inp = {k: np.asarray(v, dtype=np.float32) for k, v in inputs.items() if k != "x"}
    ws = build_wstream(inp)
    params = build_params(inp)
    consts = build_consts()
    swb, swm, nab, nam = build_bias_tables(inp)
    return {"wstream": ws, "params": params, "consts": consts, "swb": swb, "swm": swm, "nab": nab, "nam": nam}


def kernel(**inputs):
    x = np.ascontiguousarray(np.asarray(inputs["x"], dtype=np.float32))
    shared = prep_shared(inputs)
    nc = build_nc()
    in_maps = [dict(shared, x=x[b]) for b in range(NCORES)]
    res = run_bass_kernel_spmd(nc, in_maps, core_ids=list(range(NCORES)))
    return np.stack([np.asarray(r["y"], dtype=np.float32) for r in res.results], axis=0)
```

```python
import numpy as np
from contextlib import ExitStack
import concourse.bass as bass
import concourse.mybir as mybir
from concourse.bass_utils import run_bass_kernel_spmd

F32 = mybir.dt.float32
BF16 = mybir.dt.bfloat16
AF = mybir.ActivationFunctionType
ALU = mybir.AluOpType

NCORES = 8
L = 2
D = 1024
S_LEN = 2048
DFF = 2816
NT = 16
TT = 4
DC = 8
EPS = 1e-6
MASKV = -30000.0
FGROUPS = [(0, 8), (8, 8), (16, 6)]
NS = 10
CH_PER_LAYER = 2 * (22 * 2 + 3 * 8) + 12 + 8 + 24 + 8
NCH = L * CH_PER_LAYER

PL = 56
C_FFN1, C_MIX, C_FFN2, C_BG = 0, 8, 16, 24
C_NAQ, C_NAK, C_SWQ, C_SWK, C_SINK = 40, 41, 42, 43, 44
C_NAQ8, C_SWQ8, C_ESINK = 48, 49, 50
C_EPS = L * PL
NPARAM = L * PL + 8

ENGS = ("pe", "act", "dve", "pool", "sp")


class Buf:
    __slots__ = ("name", "w", "r")

    def __init__(self, name):
        self.name = name
        self.w = None
        self.r = {}


class Sched:
    def __init__(self, nc):
        self.nc = nc
        self.ops = {e: [] for e in ENGS}
        self.count = {e: 0 for e in ENGS}
        self.seen = {e: {} for e in ENGS}
        self.semkeys = list(ENGS)
        self.sems = {}

    def new_dma_sem(self, name):
        key = "d_" + name
        assert key not in self.count
        self.count[key] = 0
        self.semkeys.append(key)
        return key

    PARANOID = False

    def _waits(self, eng, reads, writes):
        need = {}
        if self.PARANOID:
            for k, c in self.count.items():
                if c > 0 and (k != eng):
                    isd = k.startswith("d_")
                    if self.PARANOID == "all" or (self.PARANOID == "dma" and isd) or (self.PARANOID == "eng" and not isd) \
                            or (self.PARANOID == "dmaw" and k.startswith("d_w")) or (self.PARANOID == "dmao" and isd and not k.startswith("d_w")):
                        need[k] = c
        for b in reads:
            if b.w is not None:
                k, c = b.w
                if c > need.get(k, 0):
                    need[k] = c
        for b in writes:
            if b.w is not None and b.w[0] != eng:
                k, c = b.w
                if c > need.get(k, 0):
                    need[k] = c
            for k, c in b.r.items():
                if k != eng and c > need.get(k, 0):
                    need[k] = c
        waits = []
        seen = self.seen[eng]
        for k, c in need.items():
            if seen.get(k, 0) < c:
                seen[k] = c
                waits.append((k, c))
        return waits

    def op(self, eng, fn, reads=(), writes=(), inc=True):
        waits = self._waits(eng, reads, writes)
        if inc:
            self.count[eng] += 1
            c = self.count[eng]
        else:
            c = self.count[eng] + 1
        for b in reads:
            if b.r.get(eng, 0) < c:
                b.r[eng] = c
        for b in writes:
            b.w = (eng, c)
            b.r = {}
        self.ops[eng].append((waits, fn, eng if inc else None, 1))

    def dma(self, eng, semkey, fn, reads=(), writes=()):
        waits = self._waits(eng, reads, writes)
        self.count[semkey] += 16
        c = self.count[semkey]
        for b in reads:
            if b.r.get(semkey, 0) < c:
                b.r[semkey] = c
        for b in writes:
            b.w = (semkey, c)
            b.r = {}
        self.ops[eng].append((waits, fn, semkey, 16))

    def final_wait(self, eng, bufs):
        waits = self._waits(eng, bufs, ())
        self.ops[eng].append((waits, None, None, 0))

    def emit(self, stack):
        nc = self.nc
        for k in self.semkeys:
            self.sems[k] = stack.enter_context(nc.semaphore(k))
        block = stack.enter_context(nc.Block())
        sems = self.sems

        def run(name):
            def body(e):
                for waits, fn, inck, incv in self.ops[name]:
                    for k, c in waits:
                        e.wait_ge(sems[k], c)
                    if fn is not None:
                        ins = fn(e)
                        if inck is not None:
                            ins.then_inc(sems[inck], incv)
            return body
        block.tensor(run("pe"))
        block.scalar(run("act"))
        block.vector(run("dve"))
        block.gpsimd(run("pool"))
        block.sync(run("sp"))


def t5_bucket(rel):
    nb = 16
    max_exact = 8
    n = np.abs(rel)
    large = max_exact + (np.log(np.maximum(n, 1) / max_exact) / np.log(128 / max_exact) * (nb - max_exact)).astype(np.int32)
    large = np.minimum(large, nb - 1)
    return ((rel > 0) * nb + np.where(n < max_exact, n, large)).astype(np.int32)


NA_PAIRS = [(6, 7, False), (4, 5, False), (4, 5, True), (2, 3, False), (0, 1, False), (-2, -1, False),
            (-4, -3, True), (-4, -3, False), (-6, -5, False)]


def na_index_tables():
    kap = np.repeat(np.arange(2), 64)[:, None]
    kc = np.tile(np.arange(64), 2)[:, None]
    qc = np.arange(64)[None, :]
    ridx = np.zeros((128, 9 * 128), np.int64)
    cidx = np.zeros((128, 9 * 128), np.int64)
    valid = np.zeros((128, 9 * 128), bool)
    qcs = np.clip(qc - 8, 0, 48)
    col_ok = (kc >= qcs) & (kc < qcs + 16)
    col_i = np.clip(kc - qc + 15, 0, 30)
    for a, (r0, r1, interior) in enumerate(NA_PAIRS):
        for par, rho in enumerate((r0, r1)):
            dr = kap - rho
            rv = (dr >= -7) & (dr <= 7)
            if interior:
                rv = rv & (dr >= -4) & (dr <= 3)
            sl = slice(a * 128 + par * 64, a * 128 + par * 64 + 64)
            ridx[:, sl] = np.broadcast_to(np.clip(dr + 7, 0, 14), (128, 64))
            cidx[:, sl] = col_i
            valid[:, sl] = rv & col_ok
    return ridx, cidx, valid


def na_tile_info(i):
    if 2 <= i <= 13:
        return i - 2, i + 2, [(0, 5, 2)]
    if i == 0:
        return 0, 3, [(0, 2, 4), (2, 2, 7)]
    if i == 1:
        return 0, 3, [(0, 3, 3), (3, 1, 7)]
    if i == 14:
        return 12, 15, [(0, 1, 1), (1, 3, 3)]
    return 12, 15, [(0, 2, 0), (2, 2, 3)]


def colchunk(W, c0, ncol=128):
    return W[:, c0:c0 + ncol].reshape(8, 128, ncol).transpose(1, 0, 2)


def build_wstream(inp):
    ws = np.zeros((NCH, 128, 8, 128), np.float32)
    n = 0

    def ffn(wg, wu, wd):
        nonlocal n
        for f0, nf in FGROUPS:
            for f in range(f0, f0 + nf):
                ws[n] = colchunk(wg, f * 128); n += 1
                ws[n] = colchunk(wu, f * 128); n += 1
            for dc in range(8):
                blk = wd[f0 * 128:(f0 + nf) * 128, dc * 128:(dc + 1) * 128].reshape(nf, 128, 128).transpose(1, 0, 2)
                ws[n, :, :nf, :] = blk; n += 1

    for l in range(L):
        ffn(inp["ffn1_w_gate"][l], inp["ffn1_w_up"][l], inp["ffn1_w_down"][l])
        win = inp["w_in"][l]
        for p in range(4):
            ws[n] = colchunk(win, p * 128); n += 1
            ws[n] = colchunk(win, 512 + p * 128); n += 1
            ws[n] = colchunk(win, 1024 + p * 128); n += 1
        for g in range(2):
            ws[n] = colchunk(win, 1536 + (2 * g) * 128); n += 1
            ws[n] = colchunk(win, 1536 + (2 * g + 1) * 128); n += 1
            kd = colchunk(win, 2048 + g * 64, 64)
            ws[n, :, :, 0:64] = kd; ws[n, :, :, 64:128] = kd; n += 1
            vd = colchunk(win, 2176 + g * 64, 64)
            ws[n, :, :, 0:64] = vd; ws[n, :, :, 64:128] = vd; n += 1
        wbr = np.concatenate([inp["w_branch_na"][l], inp["w_branch_sw"][l]], axis=0)
        for dc in range(8):
            ws[n] = colchunk(win, 2304 + dc * 128); n += 1
            ws[n] = colchunk(win, 2304 + 1024 + dc * 128); n += 1
            ws[n] = colchunk(wbr, dc * 128); n += 1
        for dc in range(8):
            ws[n] = colchunk(inp["w_out"][l], dc * 128); n += 1
        ffn(inp["ffn2_w_gate"][l], inp["ffn2_w_up"][l], inp["ffn2_w_down"][l])
    assert n == NCH
    return ws.reshape(NCH, 128, 1024)


def build_params(inp):
    P = np.zeros((128, NPARAM), np.float32)
    for l in range(L):
        o = l * PL
        P[:, o + C_FFN1:o + C_FFN1 + 8] = inp["ffn1_norm"][l].reshape(8, 128).T
        P[:, o + C_MIX:o + C_MIX + 8] = inp["mix_norm"][l].reshape(8, 128).T
        P[:, o + C_FFN2:o + C_FFN2 + 8] = inp["ffn2_norm"][l].reshape(8, 128).T
        P[:, o + C_BG:o + C_BG + 16] = inp["b_gate"][l].reshape(16, 128).T
        P[:, o + C_NAQ] = np.tile(inp["na_q_norm"][l], 2)
        P[:, o + C_NAK] = np.tile(inp["na_k_norm"][l], 2)
        P[:, o + C_SWQ] = np.tile(inp["sw_q_norm"][l], 2)
        P[:, o + C_SWK] = np.tile(inp["sw_k_norm"][l], 2)
        for pp in range(4):
            P[:64, o + C_SINK + pp] = inp["sw_sink"][l][2 * pp]
            P[64:, o + C_SINK + pp] = inp["sw_sink"][l][2 * pp + 1]
    P[:, C_EPS] = EPS
    return P


def build_consts():
    C = np.zeros((128, 384), np.float32)
    C[:, 0:128] = np.eye(128, dtype=np.float32)
    C[:, 128:256] = 1.0 / 1024.0
    C[0:64, 256:320] = 1.0 / 64.0
    C[64:128, 320:384] = 1.0 / 64.0
    return C


def build_bias_tables(inp):
    kp = np.arange(128)[:, None, None]
    c = np.arange(3)[None, :, None]
    qa = np.arange(128)[None, None, :]
    rel = (c - 1) * 128 + kp - qa
    tb = np.asarray(inp["t5_rel_table"])[t5_bucket(rel)]
    swb = np.ascontiguousarray(tb.transpose(3, 0, 1, 2)).reshape(2, 4, 128, 384).transpose(0, 2, 1, 3).reshape(2, 128, 4 * 384)
    swm = np.where(np.abs(rel) <= 128, 0.0, MASKV).astype(np.float32).reshape(128, 384)
    ridx, cidx, valid = na_index_tables()
    rpb = np.asarray(inp["na_rpb"])
    g = rpb[:, :, ridx, cidx]
    nab = np.ascontiguousarray(g.reshape(L, 4, 2, 128, 1152).transpose(0, 1, 3, 2, 4)).reshape(L, 4, 128, 2304)
    nam = np.where(valid, 0.0, MASKV).astype(np.float32)
    return np.ascontiguousarray(swb, dtype=np.float32), swm, np.ascontiguousarray(nab, dtype=np.float32), nam


P_XT = 0
P_HT = P_XT + 65536
P_RING = P_HT + 32768
P_CONST = P_RING + NS * 2048
P_PARAM = P_CONST + 1536 + 256
P_NAM = P_PARAM + NPARAM * 4
P_SWM = P_NAM + 1152 * 4
P_TMP = P_SWM + 384 * 4
P_PH = P_TMP + 6 * 2048
ARENA_BYTES = P_PH + 65536
assert ARENA_BYTES <= 212800, ARENA_BYTES
PH_ACT = 0
PH_STMP = 32768
PH_STAGE = 49152
PH_O = 0
PH_QA = 32768
PH_QB = PH_QA + 4096
PH_K = PH_QB + 4096
PH_V = PH_K + 4096
PH_TB = PH_V + 4096
PH_PT = PH_TB + 9216
PH_RD = PH_PT + 5120
PH_MRG = 32768
assert PH_RD + 2048 <= 65536


def build_nc(stop=None, dump=None):
    nc = bass.Bass("TRN2", target_bir_lowering=False)
    x_d = nc.dram_tensor("x", [S_LEN, D], F32, kind="ExternalInput").ap()
    ws_d = nc.dram_tensor("wstream", [NCH, 128, 1024], F32, kind="ExternalInput").ap()
    par_d = nc.dram_tensor("params", [128, NPARAM], F32, kind="ExternalInput").ap()
    con_d = nc.dram_tensor("consts", [128, 384], F32, kind="ExternalInput").ap()
    swb_d = nc.dram_tensor("swb", [2, 128, 1536], F32, kind="ExternalInput").ap()
    swm_d = nc.dram_tensor("swm", [128, 384], F32, kind="ExternalInput").ap()
    nab_d = nc.dram_tensor("nab", [L, 4, 128, 2304], F32, kind="ExternalInput").ap()
    nam_d = nc.dram_tensor("nam", [128, 1152], F32, kind="ExternalInput").ap()
    y_d = nc.dram_tensor("y", [S_LEN, D], F32, kind="ExternalOutput").ap()

    with ExitStack() as st:
        S = Sched(nc)
        arena = st.enter_context(nc.sbuf_tensor("arena", [128, ARENA_BYTES // 4], F32))
        ps = st.enter_context(nc.psum_tensor("ps", [128, 4096], F32))

        def f32v(off, n):
            return arena[:, off // 4: off // 4 + n]

        def bf16v(off, n):
            return arena[:, off // 4: off // 4 + n // 2].bitcast(BF16)

        bank = [ps[:, b * 512:(b + 1) * 512] for b in range(8)]
        pb = [Buf("bank%d" % b) for b in range(8)]

        xT = f32v(P_XT, 8 * 2048).rearrange("p (c t) -> p c t", c=8)
        hT = bf16v(P_HT, 8 * 2048).rearrange("p (c t) -> p c t", c=8)
        xbuf = [[Buf("x%d_%d" % (dc, tt)) for tt in range(TT)] for dc in range(DC)]
        hbuf = [Buf("h%d" % tt) for tt in range(TT)]
        slot = [bf16v(P_RING + s * 2048, 1024).rearrange("p (k j) -> p k j", k=8) for s in range(NS)]
        slot2 = [bf16v(P_RING + s * 2048, 1024) for s in range(NS)]
        slotbuf = [Buf("slot%d" % s) for s in range(NS)]
        wsem = [S.new_dma_sem("w%d" % s) for s in range(NS)]
        ident = f32v(P_CONST, 128)
        ones1k = f32v(P_CONST + 512, 128)
        blk64 = f32v(P_CONST + 1024, 128)
        onesb = bf16v(P_CONST + 1536, 128)
        constbuf = Buf("const")
        onesbbuf = Buf("onesb")
        param = f32v(P_PARAM, NPARAM)
        parambuf = Buf("param")
        nam = f32v(P_NAM, 1152)
        swm = f32v(P_SWM, 384)
        maskbuf = Buf("mask")
        tslot = [f32v(P_TMP + k * 2048, 512) for k in range(6)]
        tbuf = [Buf("tslot%d" % k) for k in range(6)]
        PH = P_PH

        def pcol(c):
            return param[:, c:c + 1]

        ring = {"load": 0, "use": 0, "rel": 0, "done": set()}

        def issue_load():
            i = ring["load"]
            if i >= NCH:
                return
            s = i % NS
            S.dma("pool", wsem[s], lambda e, i=i, s=s: e.dma_start(out=slot2[s], in_=ws_d[i]), writes=[slotbuf[s]])
            ring["load"] += 1

        def chunk():
            i = ring["use"]
            ring["use"] += 1
            assert i < ring["load"], "weight ring underflow"
            s = i % NS
            return slot[s], slotbuf[s], i

        def release(i):
            ring["done"].add(i)
            while ring["rel"] in ring["done"]:
                ring["done"].remove(ring["rel"])
                ring["rel"] += 1
                issue_load()

        def mm(out, lhsT, rhs, start, stop, reads, writes, inc):
            S.op("pe", lambda e: e.matmul(out, lhsT, rhs, start=start, stop=stop), reads=reads, writes=writes, inc=inc)

        csem = [S.new_dma_sem("c%d" % k) for k in range(4)]
        S.dma("sp", csem[0], lambda e: e.dma_start(out=f32v(P_CONST, 384), in_=con_d[:, :]), writes=[constbuf])
        S.dma("sp", csem[1], lambda e: e.dma_start(out=param, in_=par_d[:, :]), writes=[parambuf])
        S.dma("sp", csem[2], lambda e: e.dma_start(out=nam, in_=nam_d[:, :]), writes=[maskbuf])
        S.dma("sp", csem[3], lambda e: e.dma_start(out=swm, in_=swm_d[:, :]), writes=[maskbuf])
        S.op("dve", lambda e: e.memset(onesb, 1.0), writes=[onesbbuf])
        for i in range(NS):
            issue_load()
        for l in range(L):
            o = l * PL
            S.op("dve", lambda e, o=o: e.tensor_scalar(param[:, o + C_NAQ8:o + C_NAQ8 + 1], param[:, o + C_NAQ:o + C_NAQ + 1], 0.125, None, op0=ALU.mult),
                 reads=[parambuf], writes=[parambuf])
            S.op("dve", lambda e, o=o: e.tensor_scalar(param[:, o + C_SWQ8:o + C_SWQ8 + 1], param[:, o + C_SWQ:o + C_SWQ + 1], 0.125, None, op0=ALU.mult),
                 reads=[parambuf], writes=[parambuf])
            S.op("act", lambda e, o=o: e.activation(param[:, o + C_ESINK:o + C_ESINK + 4], param[:, o + C_SINK:o + C_SINK + 4], AF.Exp),
                 reads=[parambuf], writes=[parambuf])

        stage = [f32v(PH + PH_STAGE + s * 4096, 1024) for s in range(4)]
        stagebuf = [Buf("stage%d" % s) for s in range(4)]
        xsem = [S.new_dma_sem("x%d" % s) for s in range(4)]
        ysem = [S.new_dma_sem("y%d" % s) for s in range(4)]
        tu = [0]
        for i in range(NT):
            s = i % 4
            S.dma("sp", xsem[s], lambda e, i=i, s=s: e.dma_start(out=stage[s], in_=x_d[i * 128:(i + 1) * 128, :]), writes=[stagebuf[s]])
            for half in range(2):
                bk = tu[0] % 2
                tu[0] += 1
                for q in range(4):
                    dc = half * 4 + q
                    S.op("pe", lambda e, s=s, dc=dc, bk=bk, q=q: e.transpose(bank[bk][:, q * 128:(q + 1) * 128], stage[s][:, dc * 128:(dc + 1) * 128], ident),
                         reads=[stagebuf[s], constbuf], writes=[pb[bk]], inc=(q == 3))
                S.op("dve", lambda e, i=i, half=half, bk=bk: e.tensor_copy(xT[:, half * 4:half * 4 + 4, i * 128:(i + 1) * 128],
                                                                          bank[bk].rearrange("p (c t) -> p c t", c=4)),
                     writes=[pb[bk]] + [xbuf[half * 4 + q][i // 4] for q in range(4)])

        def tts(tt):
            return slice(tt * 512, (tt + 1) * 512)

        def rmsnorm(gcol0):
            for tt in range(TT):
                pst = 6 + (tt % 2)
                for dc in range(DC):
                    k = dc % 2
                    S.op("act", lambda e, dc=dc, tt=tt, k=k: e.activation(tslot[k], xT[:, dc, tts(tt)], AF.Square),
                         reads=[xbuf[dc][tt]], writes=[tbuf[k]])
                    mm(bank[pst], ones1k, tslot[k], dc == 0, dc == DC - 1, [tbuf[k], constbuf], [pb[pst]], True)
                k2 = 2 + tt % 2
                k4 = 4 + tt % 2
                S.op("act", lambda e, pst=pst, k2=k2: e.activation(tslot[k2], bank[pst], AF.Ln, bias=pcol(C_EPS)),
                     reads=[parambuf], writes=[pb[pst], tbuf[k2]])
                S.op("act", lambda e, k2=k2, k4=k4: e.activation(tslot[k4], tslot[k2], AF.Exp, scale=-0.5), reads=[tbuf[k2]], writes=[tbuf[k4]])
                for dc in range(DC):
                    S.op("dve", lambda e, dc=dc, tt=tt, k4=k4: e.scalar_tensor_tensor(hT[:, dc, tts(tt)], xT[:, dc, tts(tt)], pcol(gcol0 + dc), tslot[k4],
                                                                                     op0=ALU.mult, op1=ALU.mult),
                         reads=[xbuf[dc][tt], tbuf[k4], parambuf], writes=[hbuf[tt]])

        un = {"g": 0, "d": 0, "q": 0, "v": 0, "m": 0, "w": 0}

        def ffn():
            act = bf16v(PH + PH_ACT, 8 * 2048).rearrange("p (f t) -> p f t", f=8)
            stmp = [f32v(PH + PH_STMP + k * 2048, 512) for k in range(2)]
            for f0, nf in FGROUPS:
                for fi in range(nf):
                    G, Gb, Gi = chunk()
                    U, Ub, Ui = chunk()
                    for tt in range(TT):
                        u = un["g"] % 2
                        un["g"] += 1
                        bg, bu = u, 2 + u
                        for kc in range(DC):
                            mm(bank[bg], G[:, kc, :], hT[:, kc, tts(tt)], kc == 0, kc == DC - 1, [Gb, hbuf[tt]], [pb[bg]], kc == DC - 1)
                        for kc in range(DC):
                            mm(bank[bu], U[:, kc, :], hT[:, kc, tts(tt)], kc == 0, kc == DC - 1, [Ub, hbuf[tt]], [pb[bu]], kc == DC - 1)
                        S.op("act", lambda e, bg=bg, u=u: e.activation(stmp[u], bank[bg], AF.Silu), writes=[pb[bg], sbuf_[u]])
                        S.op("dve", lambda e, bu=bu, u=u, fi=fi, tt=tt: e.tensor_tensor(act[:, fi, tts(tt)], stmp[u], bank[bu], op=ALU.mult),
                             reads=[sbuf_[u]], writes=[pb[bu], actbuf[fi][tt]])
                    release(Gi)
                    release(Ui)
                for dc in range(DC):
                    Dk, Db, Di = chunk()
                    for tt in range(TT):
                        bd = 4 + un["d"] % 2
                        un["d"] += 1
                        for fi in range(nf):
                            mm(bank[bd], Dk[:, fi, :], act[:, fi, tts(tt)], fi == 0, fi == nf - 1, [Db, actbuf[fi][tt]], [pb[bd]], fi == nf - 1)
                        S.op("dve", lambda e, bd=bd, dc=dc, tt=tt: e.scalar_tensor_tensor(xT[:, dc, tts(tt)], bank[bd], 0.5, xT[:, dc, tts(tt)],
                                                                                         op0=ALU.mult, op1=ALU.add),
                             reads=[xbuf[dc][tt]], writes=[pb[bd], xbuf[dc][tt]])
                    release(Di)

        sbuf_ = [Buf("stmp%d" % k) for k in range(2)]
        actbuf = [[Buf("act%d_%d" % (fi, tt)) for tt in range(TT)] for fi in range(8)]

        oT = [bf16v(PH + PH_O + c * 4096, 2048) for c in range(8)]
        obuf = [[Buf("o%d_%d" % (c, tt)) for tt in range(TT)] for c in range(8)]
        qA = bf16v(PH + PH_QA, 2048)
        qB = bf16v(PH + PH_QB, 2048)
        kk = bf16v(PH + PH_K, 2048)
        vtm128 = bf16v(PH + PH_V, 2048).rearrange("p (t f) -> p t f", t=16)
        vtm64 = bf16v(PH + PH_V, 1024).rearrange("p (t f) -> p t f", t=16)
        qAbuf = [Buf("qA%d" % tt) for tt in range(TT)]
        qBbuf = [Buf("qB%d" % tt) for tt in range(TT)]
        kbuf = [Buf("k%d" % tt) for tt in range(TT)]
        vbuf = [Buf("v%d" % tt) for tt in range(TT)]
        tb = f32v(PH + PH_TB, 2304)
        tbbuf = Buf("tb")
        tbsem = S.new_dma_sem("tb")
        pT = [bf16v(PH + PH_PT + k * 1280, 640) for k in range(4)]
        pTbuf = [Buf("pT%d" % k) for k in range(4)]
        rden = [f32v(PH + PH_RD + k * 512, 128) for k in range(4)]
        rdbuf = [Buf("rd%d" % k) for k in range(4)]
        merged = bf16v(PH + PH_MRG, 8 * 2048).rearrange("p (c t) -> p c t", c=8)
        mbuf = [[Buf("m%d_%d" % (dc, tt)) for tt in range(TT)] for dc in range(DC)]

        def project_qk(W, Wb, dst, dstbuf, gcol):
            for tt in range(TT):
                u = un["q"] % 2
                un["q"] += 1
                bq, bm = u, 2 + u
                for kc in range(DC):
                    mm(bank[bq], W[:, kc, :], hT[:, kc, tts(tt)], kc == 0, kc == DC - 1, [Wb, hbuf[tt]], [pb[bq]], kc == DC - 1)
                S.op("act", lambda e, bq=bq, u=u: e.activation(tslot[u], bank[bq], AF.Square), writes=[pb[bq], tbuf[u]])
                mm(bank[bm], blk64, tslot[u], True, True, [tbuf[u], constbuf], [pb[bm]], True)
                S.op("act", lambda e, bm=bm, u=u: e.activation(tslot[2 + u], bank[bm], AF.Ln, bias=pcol(C_EPS)),
                     reads=[parambuf], writes=[pb[bm], tbuf[2 + u]])
                S.op("act", lambda e, u=u: e.activation(tslot[4 + u], tslot[2 + u], AF.Exp, scale=-0.5), reads=[tbuf[2 + u]], writes=[tbuf[4 + u]])
                S.op("dve", lambda e, bq=bq, u=u, tt=tt: e.scalar_tensor_tensor(dst[:, tts(tt)], bank[bq], pcol(gcol), tslot[4 + u], op0=ALU.mult, op1=ALU.mult),
                     reads=[tbuf[4 + u], parambuf], writes=[pb[bq], dstbuf[tt]])

        def project_v(W, Wb, nfeat):
            vt = vtm128 if nfeat == 128 else vtm64
            tpb = 512 // nfeat
            for g0 in range(0, NT, tpb):
                bv = 4 + un["v"] % 2
                un["v"] += 1
                for ti in range(tpb):
                    tile = g0 + ti
                    for kc in range(DC):
                        mm(bank[bv][:, ti * nfeat:(ti + 1) * nfeat], hT[:, kc, tile * 128:(tile + 1) * 128], W[:, kc, 0:nfeat],
                           kc == 0, kc == DC - 1, [Wb, hbuf[tile // 4]], [pb[bv]], kc == DC - 1 and ti == tpb - 1)
                S.op("act", lambda e, bv=bv, g0=g0: e.activation(vt[:, g0:g0 + tpb, :], bank[bv].rearrange("p (t f) -> p t f", t=tpb), AF.Copy),
                     writes=[pb[bv]] + sorted({vbuf[(g0 + ti) // 4] for ti in range(tpb)}, key=lambda b: b.name))

        def attn_pair(q, qbuf, kview, vfn, tiles, bias_off, sink_col, oc, sdouble=False):
            state = {}

            def qk_stage(i):
                jlo, jhi, segs = tiles(i)
                nk = jhi - jlo + 1
                def sbase(hh):
                    return (2 * hh + (i % 2)) * 512 if sdouble else hh * 1024

                def sbanks(hh):
                    if sdouble:
                        return [pb[2 * hh + (i % 2)]]
                    return [pb[2 * hh]] + ([pb[2 * hh + 1]] if nk > 4 else [])
                for kt in range(nk):
                    j = jlo + kt
                    for hh in range(2):
                        hs = slice(hh * 64, (hh + 1) * 64)
                        base = sbase(hh)
                        bk = (base + kt * 128) // 512
                        mm(ps[:, base + kt * 128: base + (kt + 1) * 128], kview[hs, j * 128:(j + 1) * 128], q[hs, i * 128:(i + 1) * 128],
                           True, True, [kbuf[j // 4], qbuf[i // 4]], [pb[bk]], kt == nk - 1 and hh == 1)
                for hh in range(2):
                    base = sbase(hh)
                    pbs = sbanks(hh)
                    for (c0, n, a0) in segs:
                        S.op("dve", lambda e, base=base, c0=c0, n=n, a0=a0, hh=hh: e.tensor_tensor(
                            ps[:, base + c0 * 128: base + (c0 + n) * 128], ps[:, base + c0 * 128: base + (c0 + n) * 128],
                            tb[:, bias_off(hh) + a0 * 128: bias_off(hh) + (a0 + n) * 128], op=ALU.add),
                            reads=[tbbuf], writes=pbs)
                    pi = un["w"] % 4
                    un["w"] += 1
                    S.op("act", lambda e, base=base, nk=nk, pi=pi: e.activation(pT[pi][:, 0:nk * 128], ps[:, base: base + nk * 128], AF.Exp),
                         writes=pbs + [pTbuf[pi]])
                    state[(i, hh)] = pi

            def pv_stage(i):
                jlo, jhi, segs = tiles(i)
                nk = jhi - jlo + 1
                bo = 5 + i % 3
                pis = [state.pop((i, hh)) for hh in range(2)]
                for kt in range(nk):
                    j = jlo + kt
                    for hh in range(2):
                        hs = slice(hh * 64, (hh + 1) * 64)
                        mm(bank[bo][hs, 0:128], vfn(j, hh), pT[pis[hh]][:, kt * 128:(kt + 1) * 128], kt == 0, kt == nk - 1,
                           [vbuf[j // 4], pTbuf[pis[hh]]], [pb[bo]], False)
                for kt in range(nk):
                    for hh in range(2):
                        hs = slice(hh * 64, (hh + 1) * 64)
                        mm(bank[bo][hs, 128:256], onesb[:, 0:64], pT[pis[hh]][:, kt * 128:(kt + 1) * 128], kt == 0, kt == nk - 1,
                           [onesbbuf, pTbuf[pis[hh]]], [pb[bo]], kt == nk - 1 and hh == 1)

            def norm_stage(i):
                bo = 5 + i % 3
                ri = i % 4
                if sink_col is not None:
                    S.op("act", lambda e, bo=bo, ri=ri: e.activation(rden[ri], bank[bo][:, 128:256], AF.Ln, bias=pcol(sink_col)),
                         reads=[parambuf], writes=[pb[bo], rdbuf[ri]])
                else:
                    S.op("act", lambda e, bo=bo, ri=ri: e.activation(rden[ri], bank[bo][:, 128:256], AF.Ln), writes=[pb[bo], rdbuf[ri]])
                S.op("act", lambda e, ri=ri: e.activation(rden[ri], rden[ri], AF.Exp, scale=-1.0), reads=[rdbuf[ri]], writes=[rdbuf[ri]])
                S.op("dve", lambda e, bo=bo, ri=ri, i=i: e.tensor_tensor(oT[oc][:, i * 128:(i + 1) * 128], bank[bo][:, 0:128], rden[ri], op=ALU.mult),
                     reads=[rdbuf[ri]], writes=[pb[bo], obuf[oc][i // 4]])

            for i in range(NT + 2):
                if i < NT:
                    qk_stage(i)
                if 1 <= i <= NT:
                    pv_stage(i - 1)
                if i >= 2:
                    norm_stage(i - 2)

        def sw_tiles(n):
            lo, hi = max(n - 1, 0), min(n + 1, NT - 1)
            return lo, hi, [(0, hi - lo + 1, lo - n + 1)]

        tb_alias = [stagebuf[0], stagebuf[1], stagebuf[2]] + [mbuf[dc][tt] for dc in (4, 5, 6) for tt in range(TT)]

        def mixer(l):
            o = l * PL
            for p in range(4):
                Q, Qb, Qi = chunk()
                K, Kb, Ki = chunk()
                V, Vb, Vi = chunk()
                project_qk(Q, Qb, qA, qAbuf, o + C_NAQ8)
                release(Qi)
                project_qk(K, Kb, kk, kbuf, o + C_NAK)
                release(Ki)
                project_v(V, Vb, 128)
                release(Vi)
                if stop == (l, "proj"):
                    for tt in range(TT):
                        S.op("dve", lambda e, tt=tt: e.tensor_copy(xT[:, 0, tts(tt)], qA[:, tts(tt)]), reads=[qAbuf[tt]], writes=[xbuf[0][tt]])
                        S.op("dve", lambda e, tt=tt: e.tensor_copy(xT[:, 1, tts(tt)], kk[:, tts(tt)]), reads=[kbuf[tt]], writes=[xbuf[1][tt]])
                    return
                S.dma("sp", tbsem, lambda e, l=l, p=p: e.dma_start(out=tb[:, 0:2304], in_=nab_d[l, p]), writes=[tbbuf] + tb_alias)
                for hh in range(2):
                    S.op("dve", lambda e, hh=hh: e.tensor_tensor(tb[:, hh * 1152:(hh + 1) * 1152], tb[:, hh * 1152:(hh + 1) * 1152], nam, op=ALU.add),
                         reads=[tbbuf, maskbuf], writes=[tbbuf])
                attn_pair(qA, qAbuf, kk, lambda j, hh: vtm128[:, j, hh * 64:(hh + 1) * 64], na_tile_info,
                          lambda hh: hh * 1152, None, p)
            for g in range(2):
                Q0, Q0b, Q0i = chunk()
                Q1, Q1b, Q1i = chunk()
                K, Kb, Ki = chunk()
                V, Vb, Vi = chunk()
                project_qk(Q0, Q0b, qA, qAbuf, o + C_SWQ8)
                release(Q0i)
                project_qk(Q1, Q1b, qB, qBbuf, o + C_SWQ8)
                release(Q1i)
                project_qk(K, Kb, kk, kbuf, o + C_SWK)
                release(Ki)
                project_v(V, Vb, 64)
                release(Vi)
                S.dma("sp", tbsem, lambda e, g=g: e.dma_start(out=tb[:, 0:1536], in_=swb_d[g]), writes=[tbbuf] + tb_alias)
                for h4 in range(4):
                    S.op("dve", lambda e, h4=h4: e.tensor_tensor(tb[:, h4 * 384:(h4 + 1) * 384], tb[:, h4 * 384:(h4 + 1) * 384], swm, op=ALU.add),
                         reads=[tbbuf, maskbuf], writes=[tbbuf])
                for pp in range(2):
                    attn_pair(qA if pp == 0 else qB, qAbuf if pp == 0 else qBbuf, kk, lambda j, hh: vtm64[:, j, :], sw_tiles,
                              lambda hh, pp=pp: (2 * pp + hh) * 384, o + C_ESINK + 2 * g + pp, 4 + 2 * g + pp, sdouble=True)
            for dc in range(DC):
                GA, GAb, GAi = chunk()
                GB, GBb, GBi = chunk()
                BR, BRb, BRi = chunk()
                for tt in range(TT):
                    u = un["m"] % 2
                    un["m"] += 1
                    b0 = 4 * u
                    for kc in range(DC):
                        mm(bank[b0], GA[:, kc, :], hT[:, kc, tts(tt)], kc == 0, kc == DC - 1, [GAb, hbuf[tt]], [pb[b0]], kc == DC - 1)
                    for kc in range(DC):
                        mm(bank[b0 + 1], GB[:, kc, :], hT[:, kc, tts(tt)], kc == 0, kc == DC - 1, [GBb, hbuf[tt]], [pb[b0 + 1]], kc == DC - 1)
                    for c in range(4):
                        mm(bank[b0 + 2], BR[:, c, :], oT[c][:, tts(tt)], c == 0, c == 3, [BRb, obuf[c][tt]], [pb[b0 + 2]], c == 3)
                    for c in range(4):
                        mm(bank[b0 + 3], BR[:, 4 + c, :], oT[4 + c][:, tts(tt)], c == 0, c == 3, [BRb, obuf[4 + c][tt]], [pb[b0 + 3]], c == 3)
                    tA, tB = tslot[u], tslot[2 + u]
                    S.op("act", lambda e, b0=b0, tA=tA, dc=dc: e.activation(tA, bank[b0], AF.Sigmoid, bias=pcol(o + C_BG + dc)),
                         reads=[parambuf], writes=[pb[b0], tbuf[u]])
                    S.op("act", lambda e, b0=b0, tB=tB, dc=dc: e.activation(tB, bank[b0 + 1], AF.Sigmoid, bias=pcol(o + C_BG + 8 + dc)),
                         reads=[parambuf], writes=[pb[b0 + 1], tbuf[2 + u]])
                    S.op("dve", lambda e, b0=b0, tA=tA: e.tensor_tensor(tA, tA, bank[b0 + 2], op=ALU.mult), reads=[tbuf[u]], writes=[pb[b0 + 2], tbuf[u]])
                    S.op("dve", lambda e, b0=b0, tB=tB: e.tensor_tensor(tB, tB, bank[b0 + 3], op=ALU.mult), reads=[tbuf[2 + u]], writes=[pb[b0 + 3], tbuf[2 + u]])
                    S.op("dve", lambda e, tA=tA, tB=tB, dc=dc, tt=tt: e.tensor_tensor(merged[:, dc, tts(tt)], tA, tB, op=ALU.add),
                         reads=[tbuf[u], tbuf[2 + u]], writes=[mbuf[dc][tt]])
                release(GAi)
                release(GBi)
                release(BRi)
            for dcp in range(DC):
                WO, WOb, WOi = chunk()
                for tt in range(TT):
                    bw = un["d"] % 2 + 4
                    un["d"] += 1
                    for dc in range(DC):
                        mm(bank[bw], WO[:, dc, :], merged[:, dc, tts(tt)], dc == 0, dc == DC - 1, [WOb, mbuf[dc][tt]], [pb[bw]], dc == DC - 1)
                    S.op("dve", lambda e, bw=bw, dcp=dcp, tt=tt: e.tensor_tensor(xT[:, dcp, tts(tt)], bank[bw], xT[:, dcp, tts(tt)], op=ALU.add),
                         reads=[xbuf[dcp][tt]], writes=[pb[bw], xbuf[dcp][tt]])
                release(WOi)

        def run():
            if stop == (0, "load"):
                return
            for l in range(L):
                o = l * PL
                rmsnorm(o + C_FFN1)
                ffn()
                if stop == (l, "ffn1"):
                    return
                rmsnorm(o + C_MIX)
                if stop == (l, "norm"):
                    return
                mixer(l)
                if stop == (l, "proj"):
                    return
                if stop == (l, "mixer"):
                    return
                rmsnorm(o + C_FFN2)
                ffn()
                if stop == (l, "ffn2"):
                    return
        run()

        if dump == "h":
            for dc in range(DC):
                for tt in range(TT):
                    S.op("dve", lambda e, dc=dc, tt=tt: e.tensor_copy(xT[:, dc, tts(tt)], hT[:, dc, tts(tt)]), reads=[hbuf[tt]], writes=[xbuf[dc][tt]])
        if dump == "o":
            for c in range(8):
                for tt in range(TT):
                    S.op("dve", lambda e, c=c, tt=tt: e.tensor_copy(xT[:, c, tts(tt)], oT[c][:, tts(tt)]), reads=[obuf[c][tt]], writes=[xbuf[c][tt]])

        ybufs = []
        for i in range(NT):
            s = i % 4
            for half in range(2):
                bk = tu[0] % 2
                tu[0] += 1
                for q in range(4):
                    dc = half * 4 + q
                    S.op("pe", lambda e, i=i, dc=dc, bk=bk, q=q: e.transpose(bank[bk][:, q * 128:(q + 1) * 128], xT[:, dc, i * 128:(i + 1) * 128], ident),
                         reads=[xbuf[dc][i // 4], constbuf], writes=[pb[bk]], inc=(q == 3))
                S.op("dve", lambda e, s=s, half=half, bk=bk: e.tensor_copy(stage[s][:, half * 512:(half + 1) * 512], bank[bk]),
                     writes=[pb[bk], stagebuf[s]])
            yb = Buf("y%d" % i)
            ybufs.append(yb)
            S.dma("sp", ysem[s], lambda e, i=i, s=s: e.dma_start(out=y_d[i * 128:(i + 1) * 128, :], in_=stage[s]), reads=[stagebuf[s]], writes=[yb])
        S.final_wait("sp", ybufs)
        S.emit(st)
    return nc


_HOST_CACHE = {}


def prep_shared(inputs):
    inp = {k: np.asarray(v, dtype=np.float32) for k, v in inputs.items() if k != "x"}
    ws = build_wstream(inp)
    params = build_params(inp)
    consts = build_consts()
    swb, swm, nab, nam = build_bias_tables(inp)
    return {"wstream": ws, "params": params, "consts": consts, "swb": swb, "swm": swm, "nab": nab, "nam": nam}


def kernel(**inputs):
    x = np.ascontiguousarray(np.asarray(inputs["x"], dtype=np.float32))
    shared = prep_shared(inputs)
    nc = build_nc()
    in_maps = [dict(shared, x=x[b]) for b in range(NCORES)]
    res = run_bass_kernel_spmd(nc, in_maps, core_ids=list(range(NCORES)))
    return np.stack([np.asarray(r["y"], dtype=np.float32) for r in res.results], axis=0)
```

```python
import numpy as np
from contextlib import ExitStack
import concourse.bass as bass
import concourse.mybir as mybir
from concourse.bass_utils import run_bass_kernel_spmd

F32 = mybir.dt.float32
BF16 = mybir.dt.bfloat16
AF = mybir.ActivationFunctionType
ALU = mybir.AluOpType

NCORES = 8
L = 2
D = 1024
S_LEN = 2048
DFF = 2816
NT = 16
TT = 4
DC = 8
EPS = 1e-6
MASKV = -30000.0
FGROUPS = [(0, 8), (8, 8), (16, 6)]
NS = 10
CH_PER_LAYER = 2 * (22 * 2 + 3 * 8) + 12 + 8 + 24 + 8
NCH = L * CH_PER_LAYER

PL = 56
C_FFN1, C_MIX, C_FFN2, C_BG = 0, 8, 16, 24
C_NAQ, C_NAK, C_SWQ, C_SWK, C_SINK = 40, 41, 42, 43, 44
C_NAQ8, C_SWQ8, C_ESINK = 48, 49, 50
C_EPS = L * PL
NPARAM = L * PL + 8

ENGS = ("pe", "act", "dve", "pool", "sp")


class Buf:
    __slots__ = ("name", "w", "r")

    def __init__(self, name):
        self.name = name
        self.w = None
        self.r = {}


class Sched:
    def __init__(self, nc):
        self.nc = nc
        self.ops = {e: [] for e in ENGS}
        self.count = {e: 0 for e in ENGS}
        self.seen = {e: {} for e in ENGS}
        self.semkeys = list(ENGS)
        self.sems = {}

    def new_dma_sem(self, name):
        key = "d_" + name
        assert key not in self.count
        self.count[key] = 0
        self.semkeys.append(key)
        return key

    PARANOID = False

    def _waits(self, eng, reads, writes):
        need = {}
        if self.PARANOID:
            for k, c in self.count.items():
                if c > 0 and (k != eng):
                    isd = k.startswith("d_")
                    if self.PARANOID == "all" or (self.PARANOID == "dma" and isd) or (self.PARANOID == "eng" and not isd) \
                            or (self.PARANOID == "dmaw" and k.startswith("d_w")) or (self.PARANOID == "dmao" and isd and not k.startswith("d_w")):
                        need[k] = c
        for b in reads:
            if b.w is not None:
                k, c = b.w
                if c > need.get(k, 0):
                    need[k] = c
        for b in writes:
            if b.w is not None and b.w[0] != eng:
                k, c = b.w
                if c > need.get(k, 0):
                    need[k] = c
            for k, c in b.r.items():
                if k != eng and c > need.get(k, 0):
                    need[k] = c
        waits = []
        seen = self.seen[eng]
        for k, c in need.items():
            if seen.get(k, 0) < c:
                seen[k] = c
                waits.append((k, c))
        return waits

    def op(self, eng, fn, reads=(), writes=(), inc=True):
        waits = self._waits(eng, reads, writes)
        if inc:
            self.count[eng] += 1
            c = self.count[eng]
        else:
            c = self.count[eng] + 1
        for b in reads:
            if b.r.get(eng, 0) < c:
                b.r[eng] = c
        for b in writes:
            b.w = (eng, c)
            b.r = {}
        self.ops[eng].append((waits, fn, eng if inc else None, 1))

    def dma(self, eng, semkey, fn, reads=(), writes=()):
        waits = self._waits(eng, reads, writes)
        self.count[semkey] += 16
        c = self.count[semkey]
        for b in reads:
            if b.r.get(semkey, 0) < c:
                b.r[semkey] = c
        for b in writes:
            b.w = (semkey, c)
            b.r = {}
        self.ops[eng].append((waits, fn, semkey, 16))

    def final_wait(self, eng, bufs):
        waits = self._waits(eng, bufs, ())
        self.ops[eng].append((waits, None, None, 0))

    def emit(self, stack):
        nc = self.nc
        for k in self.semkeys:
            self.sems[k] = stack.enter_context(nc.semaphore(k))
        block = stack.enter_context(nc.Block())
        sems = self.sems

        def run(name):
            def body(e):
                for waits, fn, inck, incv in self.ops[name]:
                    for k, c in waits:
                        e.wait_ge(sems[k], c)
                    if fn is not None:
                        ins = fn(e)
                        if inck is not None:
                            ins.then_inc(sems[inck], incv)
            return body
        block.tensor(run("pe"))
        block.scalar(run("act"))
        block.vector(run("dve"))
        block.gpsimd(run("pool"))
        block.sync(run("sp"))


def t5_bucket(rel):
    nb = 16
    max_exact = 8
    n = np.abs(rel)
    large = max_exact + (np.log(np.maximum(n, 1) / max_exact) / np.log(128 / max_exact) * (nb - max_exact)).astype(np.int32)
    large = np.minimum(large, nb - 1)
    return ((rel > 0) * nb + np.where(n < max_exact, n, large)).astype(np.int32)


NA_PAIRS = [(6, 7, False), (4, 5, False), (4, 5, True), (2, 3, False), (0, 1, False), (-2, -1, False),
            (-4, -3, True), (-4, -3, False), (-6, -5, False)]


def na_index_tables():
    kap = np.repeat(np.arange(2), 64)[:, None]
    kc = np.tile(np.arange(64), 2)[:, None]
    qc = np.arange(64)[None, :]
    ridx = np.zeros((128, 9 * 128), np.int64)
    cidx = np.zeros((128, 9 * 128), np.int64)
    valid = np.zeros((128, 9 * 128), bool)
    qcs = np.clip(qc - 8, 0, 48)
    col_ok = (kc >= qcs) & (kc < qcs + 16)
    col_i = np.clip(kc - qc + 15, 0, 30)
    for a, (r0, r1, interior) in enumerate(NA_PAIRS):
        for par, rho in enumerate((r0, r1)):
            dr = kap - rho
            rv = (dr >= -7) & (dr <= 7)
            if interior:
                rv = rv & (dr >= -4) & (dr <= 3)
            sl = slice(a * 128 + par * 64, a * 128 + par * 64 + 64)
            ridx[:, sl] = np.broadcast_to(np.clip(dr + 7, 0, 14), (128, 64))
            cidx[:, sl] = col_i
            valid[:, sl] = rv & col_ok
    return ridx, cidx, valid


def na_tile_info(i):
    if 2 <= i <= 13:
        return i - 2, i + 2, [(0, 5, 2)]
    if i == 0:
        return 0, 3, [(0, 2, 4), (2, 2, 7)]
    if i == 1:
        return 0, 3, [(0, 3, 3), (3, 1, 7)]
    if i == 14:
        return 12, 15, [(0, 1, 1), (1, 3, 3)]
    return 12, 15, [(0, 2, 0), (2, 2, 3)]


def colchunk(W, c0, ncol=128):
    return W[:, c0:c0 + ncol].reshape(8, 128, ncol).transpose(1, 0, 2)


def build_wstream(inp):
    ws = np.zeros((NCH, 128, 8, 128), np.float32)
    n = 0

    def ffn(wg, wu, wd):
        nonlocal n
        for f0, nf in FGROUPS:
            for f in range(f0, f0 + nf):
                ws[n] = colchunk(wg, f * 128); n += 1
                ws[n] = colchunk(wu, f * 128); n += 1
            for dc in range(8):
                blk = wd[f0 * 128:(f0 + nf) * 128, dc * 128:(dc + 1) * 128].reshape(nf, 128, 128).transpose(1, 0, 2)
                ws[n, :, :nf, :] = blk; n += 1

    for l in range(L):
        ffn(inp["ffn1_w_gate"][l], inp["ffn1_w_up"][l], inp["ffn1_w_down"][l])
        win = inp["w_in"][l]
        for p in range(4):
            ws[n] = colchunk(win, p * 128); n += 1
            ws[n] = colchunk(win, 512 + p * 128); n += 1
            ws[n] = colchunk(win, 1024 + p * 128); n += 1
        for g in range(2):
            ws[n] = colchunk(win, 1536 + (2 * g) * 128); n += 1
            ws[n] = colchunk(win, 1536 + (2 * g + 1) * 128); n += 1
            kd = colchunk(win, 2048 + g * 64, 64)
            ws[n, :, :, 0:64] = kd; ws[n, :, :, 64:128] = kd; n += 1
            vd = colchunk(win, 2176 + g * 64, 64)
            ws[n, :, :, 0:64] = vd; ws[n, :, :, 64:128] = vd; n += 1
        wbr = np.concatenate([inp["w_branch_na"][l], inp["w_branch_sw"][l]], axis=0)
        for dc in range(8):
            ws[n] = colchunk(win, 2304 + dc * 128); n += 1
            ws[n] = colchunk(win, 2304 + 1024 + dc * 128); n += 1
            ws[n] = colchunk(wbr, dc * 128); n += 1
        for dc in range(8):
            ws[n] = colchunk(inp["w_out"][l], dc * 128); n += 1
        ffn(inp["ffn2_w_gate"][l], inp["ffn2_w_up"][l], inp["ffn2_w_down"][l])
    assert n == NCH
    return ws.reshape(NCH, 128, 1024)


def build_params(inp):
    P = np.zeros((128, NPARAM), np.float32)
    for l in range(L):
        o = l * PL
        P[:, o + C_FFN1:o + C_FFN1 + 8] = inp["ffn1_norm"][l].reshape(8, 128).T
        P[:, o + C_MIX:o + C_MIX + 8] = inp["mix_norm"][l].reshape(8, 128).T
        P[:, o + C_FFN2:o + C_FFN2 + 8] = inp["ffn2_norm"][l].reshape(8, 128).T
        P[:, o + C_BG:o + C_BG + 16] = inp["b_gate"][l].reshape(16, 128).T
        P[:, o + C_NAQ] = np.tile(inp["na_q_norm"][l], 2)
        P[:, o + C_NAK] = np.tile(inp["na_k_norm"][l], 2)
        P[:, o + C_SWQ] = np.tile(inp["sw_q_norm"][l], 2)
        P[:, o + C_SWK] = np.tile(inp["sw_k_norm"][l], 2)
        for pp in range(4):
            P[:64, o + C_SINK + pp] = inp["sw_sink"][l][2 * pp]
            P[64:, o + C_SINK + pp] = inp["sw_sink"][l][2 * pp + 1]
    P[:, C_EPS] = EPS
    return P


def build_consts():
    C = np.zeros((128, 384), np.float32)
    C[:, 0:128] = np.eye(128, dtype=np.float32)
    C[:, 128:256] = 1.0 / 1024.0
    C[0:64, 256:320] = 1.0 / 64.0
    C[64:128, 320:384] = 1.0 / 64.0
    return C


def build_bias_tables(inp):
    kp = np.arange(128)[:, None, None]
    c = np.arange(3)[None, :, None]
    qa = np.arange(128)[None, None, :]
    rel = (c - 1) * 128 + kp - qa
    tb = np.asarray(inp["t5_rel_table"])[t5_bucket(rel)]
    swb = np.ascontiguousarray(tb.transpose(3, 0, 1, 2)).reshape(2, 4, 128, 384).transpose(0, 2, 1, 3).reshape(2, 128, 4 * 384)
    swm = np.where(np.abs(rel) <= 128, 0.0, MASKV).astype(np.float32).reshape(128, 384)
    ridx, cidx, valid = na_index_tables()
    rpb = np.asarray(inp["na_rpb"])
    g = rpb[:, :, ridx, cidx]
    nab = np.ascontiguousarray(g.reshape(L, 4, 2, 128, 1152).transpose(0, 1, 3, 2, 4)).reshape(L, 4, 128, 2304)
    nam = np.where(valid, 0.0, MASKV).astype(np.float32)
    return np.ascontiguousarray(swb, dtype=np.float32), swm, np.ascontiguousarray(nab, dtype=np.float32), nam


P_XT = 0
P_HT = P_XT + 65536
P_RING = P_HT + 32768
P_CONST = P_RING + NS * 2048
P_PARAM = P_CONST + 1536 + 256
P_NAM = P_PARAM + NPARAM * 4
P_SWM = P_NAM + 1152 * 4
P_TMP = P_SWM + 384 * 4
P_PH = P_TMP + 6 * 2048
P_PC = P_PH + 65536
ARENA_BYTES = P_PC + 2048
assert ARENA_BYTES <= 212800, ARENA_BYTES
PH_ACT = 0
PH_STMP = 32768
PH_STAGE = 49152
PH_O = 0
PH_QA = 32768
PH_QB = PH_QA + 4096
PH_K = PH_QB + 4096
PH_V = PH_K + 4096
PH_TB = PH_V + 4096
PH_PT = PH_TB + 9216
PH_RD = PH_PT + 5120
PH_MRG = 32768
assert PH_RD + 2048 <= 65536


def build_nc(stop=None, dump=None):
    nc = bass.Bass("TRN2", target_bir_lowering=False)
    x_d = nc.dram_tensor("x", [S_LEN, D], F32, kind="ExternalInput").ap()
    ws_d = nc.dram_tensor("wstream", [NCH, 128, 1024], F32, kind="ExternalInput").ap()
    par_d = nc.dram_tensor("params", [128, NPARAM], F32, kind="ExternalInput").ap()
    con_d = nc.dram_tensor("consts", [128, 384], F32, kind="ExternalInput").ap()
    swb_d = nc.dram_tensor("swb", [2, 128, 1536], F32, kind="ExternalInput").ap()
    swm_d = nc.dram_tensor("swm", [128, 384], F32, kind="ExternalInput").ap()
    nab_d = nc.dram_tensor("nab", [L, 4, 128, 2304], F32, kind="ExternalInput").ap()
    nam_d = nc.dram_tensor("nam", [128, 1152], F32, kind="ExternalInput").ap()
    y_d = nc.dram_tensor("y", [S_LEN, D], F32, kind="ExternalOutput").ap()

    with ExitStack() as st:
        S = Sched(nc)
        arena = st.enter_context(nc.sbuf_tensor("arena", [128, ARENA_BYTES // 4], F32))
        ps = st.enter_context(nc.psum_tensor("ps", [128, 4096], F32))

        def f32v(off, n):
            return arena[:, off // 4: off // 4 + n]

        def bf16v(off, n):
            return arena[:, off // 4: off // 4 + n // 2].bitcast(BF16)

        bank = [ps[:, b * 512:(b + 1) * 512] for b in range(8)]
        pb = [Buf("bank%d" % b) for b in range(8)]

        xT = f32v(P_XT, 8 * 2048).rearrange("p (c t) -> p c t", c=8)
        hT = bf16v(P_HT, 8 * 2048).rearrange("p (c t) -> p c t", c=8)
        xbuf = [[Buf("x%d_%d" % (dc, tt)) for tt in range(TT)] for dc in range(DC)]
        hbuf = [Buf("h%d" % tt) for tt in range(TT)]
        slot = [bf16v(P_RING + s * 2048, 1024).rearrange("p (k j) -> p k j", k=8) for s in range(NS)]
        slot2 = [bf16v(P_RING + s * 2048, 1024) for s in range(NS)]
        slotbuf = [Buf("slot%d" % s) for s in range(NS)]
        wsem = [S.new_dma_sem("w%d" % s) for s in range(NS)]
        ident = f32v(P_CONST, 128)
        ones1k = f32v(P_CONST + 512, 128)
        blk64 = f32v(P_CONST + 1024, 128)
        onesb = bf16v(P_CONST + 1536, 128)
        constbuf = Buf("const")
        onesbbuf = Buf("onesb")
        param = f32v(P_PARAM, NPARAM)
        parambuf = Buf("param")
        nam = f32v(P_NAM, 1152)
        swm = f32v(P_SWM, 384)
        maskbuf = Buf("mask")
        tslot = [f32v(P_TMP + k * 2048, 512) for k in range(6)]
        tbuf = [Buf("tslot%d" % k) for k in range(6)]
        PH = P_PH

        def pcol(c):
            return param[:, c:c + 1]

        ring = {"load": 0, "use": 0, "rel": 0, "done": set()}

        def issue_load():
            i = ring["load"]
            if i >= NCH:
                return
            s = i % NS
            S.dma("pool", wsem[s], lambda e, i=i, s=s: e.dma_start(out=slot2[s], in_=ws_d[i]), writes=[slotbuf[s]])
            ring["load"] += 1

        def chunk():
            i = ring["use"]
            ring["use"] += 1
            assert i < ring["load"], "weight ring underflow"
            s = i % NS
            return slot[s], slotbuf[s], i

        def release(i):
            ring["done"].add(i)
            while ring["rel"] in ring["done"]:
                ring["done"].remove(ring["rel"])
                ring["rel"] += 1
                issue_load()

        def mm(out, lhsT, rhs, start, stop, reads, writes, inc):
            S.op("pe", lambda e: e.matmul(out, lhsT, rhs, start=start, stop=stop), reads=reads, writes=writes, inc=inc)

        csem = [S.new_dma_sem("c%d" % k) for k in range(4)]
        S.dma("sp", csem[0], lambda e: e.dma_start(out=f32v(P_CONST, 384), in_=con_d[:, :]), writes=[constbuf])
        S.dma("sp", csem[1], lambda e: e.dma_start(out=param, in_=par_d[:, :]), writes=[parambuf])
        S.dma("sp", csem[2], lambda e: e.dma_start(out=nam, in_=nam_d[:, :]), writes=[maskbuf])
        S.dma("sp", csem[3], lambda e: e.dma_start(out=swm, in_=swm_d[:, :]), writes=[maskbuf])
        S.op("dve", lambda e: e.memset(onesb, 1.0), writes=[onesbbuf])
        for i in range(NS):
            issue_load()
        for l in range(L):
            o = l * PL
            S.op("dve", lambda e, o=o: e.tensor_scalar(param[:, o + C_NAQ8:o + C_NAQ8 + 1], param[:, o + C_NAQ:o + C_NAQ + 1], 0.125, None, op0=ALU.mult),
                 reads=[parambuf], writes=[parambuf])
            S.op("dve", lambda e, o=o: e.tensor_scalar(param[:, o + C_SWQ8:o + C_SWQ8 + 1], param[:, o + C_SWQ:o + C_SWQ + 1], 0.125, None, op0=ALU.mult),
                 reads=[parambuf], writes=[parambuf])
            S.op("act", lambda e, o=o: e.activation(param[:, o + C_ESINK:o + C_ESINK + 4], param[:, o + C_SINK:o + C_SINK + 4], AF.Exp),
                 reads=[parambuf], writes=[parambuf])

        stage = [f32v(PH + PH_STAGE + s * 4096, 1024) for s in range(4)]
        stagebuf = [Buf("stage%d" % s) for s in range(4)]
        xsem = [S.new_dma_sem("x%d" % s) for s in range(4)]
        ysem = [S.new_dma_sem("y%d" % s) for s in range(4)]
        tu = [0]
        for i in range(NT):
            s = i % 4
            S.dma("sp", xsem[s], lambda e, i=i, s=s: e.dma_start(out=stage[s], in_=x_d[i * 128:(i + 1) * 128, :]), writes=[stagebuf[s]])
            for half in range(2):
                bk = tu[0] % 2
                tu[0] += 1
                for q in range(4):
                    dc = half * 4 + q
                    S.op("pe", lambda e, s=s, dc=dc, bk=bk, q=q: e.transpose(bank[bk][:, q * 128:(q + 1) * 128], stage[s][:, dc * 128:(dc + 1) * 128], ident),
                         reads=[stagebuf[s], constbuf], writes=[pb[bk]], inc=(q == 3))
                S.op("dve", lambda e, i=i, half=half, bk=bk: e.tensor_copy(xT[:, half * 4:half * 4 + 4, i * 128:(i + 1) * 128],
                                                                          bank[bk].rearrange("p (c t) -> p c t", c=4)),
                     writes=[pb[bk]] + [xbuf[half * 4 + q][i // 4] for q in range(4)])

        def tts(tt):
            return slice(tt * 512, (tt + 1) * 512)

        def rmsnorm(gcol0):
            for tt in range(TT):
                pst = 6 + (tt % 2)
                for dc in range(DC):
                    k = dc % 2
                    S.op("act", lambda e, dc=dc, tt=tt, k=k: e.activation(tslot[k], xT[:, dc, tts(tt)], AF.Square),
                         reads=[xbuf[dc][tt]], writes=[tbuf[k]])
                    mm(bank[pst], ones1k, tslot[k], dc == 0, dc == DC - 1, [tbuf[k], constbuf], [pb[pst]], True)
                k2 = 2 + tt % 2
                k4 = 4 + tt % 2
                S.op("act", lambda e, pst=pst, k2=k2: e.activation(tslot[k2], bank[pst], AF.Ln, bias=pcol(C_EPS)),
                     reads=[parambuf], writes=[pb[pst], tbuf[k2]])
                S.op("act", lambda e, k2=k2, k4=k4: e.activation(tslot[k4], tslot[k2], AF.Exp, scale=-0.5), reads=[tbuf[k2]], writes=[tbuf[k4]])
                for dc in range(DC):
                    S.op("dve", lambda e, dc=dc, tt=tt, k4=k4: e.scalar_tensor_tensor(hT[:, dc, tts(tt)], xT[:, dc, tts(tt)], pcol(gcol0 + dc), tslot[k4],
                                                                                     op0=ALU.mult, op1=ALU.mult),
                         reads=[xbuf[dc][tt], tbuf[k4], parambuf], writes=[hbuf[tt]])

        un = {"g": 0, "d": 0, "q": 0, "v": 0, "m": 0, "w": 0}

        def ffn():
            act = bf16v(PH + PH_ACT, 8 * 2048).rearrange("p (f t) -> p f t", f=8)
            stmp = [f32v(PH + PH_STMP + k * 2048, 512) for k in range(2)]
            for f0, nf in FGROUPS:
                for fi in range(nf):
                    G, Gb, Gi = chunk()
                    U, Ub, Ui = chunk()
                    for tt in range(TT):
                        u = un["g"] % 2
                        un["g"] += 1
                        bg, bu = u, 2 + u
                        for kc in range(DC):
                            mm(bank[bg], G[:, kc, :], hT[:, kc, tts(tt)], kc == 0, kc == DC - 1, [Gb, hbuf[tt]], [pb[bg]], kc == DC - 1)
                        for kc in range(DC):
                            mm(bank[bu], U[:, kc, :], hT[:, kc, tts(tt)], kc == 0, kc == DC - 1, [Ub, hbuf[tt]], [pb[bu]], kc == DC - 1)
                        S.op("act", lambda e, bg=bg, u=u: e.activation(stmp[u], bank[bg], AF.Silu), writes=[pb[bg], sbuf_[u]])
                        S.op("dve", lambda e, bu=bu, u=u, fi=fi, tt=tt: e.tensor_tensor(act[:, fi, tts(tt)], stmp[u], bank[bu], op=ALU.mult),
                             reads=[sbuf_[u]], writes=[pb[bu], actbuf[fi][tt]])
                    release(Gi)
                    release(Ui)
                for dc in range(DC):
                    Dk, Db, Di = chunk()
                    for tt in range(TT):
                        bd = 4 + un["d"] % 2
                        un["d"] += 1
                        for fi in range(nf):
                            mm(bank[bd], Dk[:, fi, :], act[:, fi, tts(tt)], fi == 0, fi == nf - 1, [Db, actbuf[fi][tt]], [pb[bd]], fi == nf - 1)
                        S.op("dve", lambda e, bd=bd, dc=dc, tt=tt: e.scalar_tensor_tensor(xT[:, dc, tts(tt)], bank[bd], 0.5, xT[:, dc, tts(tt)],
                                                                                         op0=ALU.mult, op1=ALU.add),
                             reads=[xbuf[dc][tt]], writes=[pb[bd], xbuf[dc][tt]])
                    release(Di)

        sbuf_ = [Buf("stmp%d" % k) for k in range(2)]
        actbuf = [[Buf("act%d_%d" % (fi, tt)) for tt in range(TT)] for fi in range(8)]

        oT = [bf16v(PH + PH_O + c * 4096, 2048) for c in range(8)]
        obuf = [[Buf("o%d_%d" % (c, tt)) for tt in range(TT)] for c in range(8)]
        qA = bf16v(PH + PH_QA, 2048)
        qB = bf16v(PH + PH_QB, 2048)
        kk = bf16v(PH + PH_K, 2048)
        vtm128 = bf16v(PH + PH_V, 2048).rearrange("p (t f) -> p t f", t=16)
        vtm64 = bf16v(PH + PH_V, 1024).rearrange("p (t f) -> p t f", t=16)
        qAbuf = [Buf("qA%d" % tt) for tt in range(TT)]
        qBbuf = [Buf("qB%d" % tt) for tt in range(TT)]
        kbuf = [Buf("k%d" % tt) for tt in range(TT)]
        vbuf = [Buf("v%d" % tt) for tt in range(TT)]
        tb = f32v(PH + PH_TB, 2304)
        tbbuf = Buf("tb")
        tbsem = S.new_dma_sem("tb")
        pT = [bf16v(PH + PH_PT + k * 1280, 640) for k in range(4)]
        pTbuf = [Buf("pT%d" % k) for k in range(4)]
        rden = [f32v(PH + PH_RD + k * 512, 128) for k in range(4)]
        rdbuf = [Buf("rd%d" % k) for k in range(4)]
        pcop = [f32v(P_PC + k * 512, 128) for k in range(4)]
        pcbuf = [Buf("pc%d" % k) for k in range(4)]
        merged = bf16v(PH + PH_MRG, 8 * 2048).rearrange("p (c t) -> p c t", c=8)
        mbuf = [[Buf("m%d_%d" % (dc, tt)) for tt in range(TT)] for dc in range(DC)]

        def project_qk(W, Wb, dst, dstbuf, gcol):
            for tt in range(TT):
                u = un["q"] % 2
                un["q"] += 1
                bq, bm = u, 2 + u
                for kc in range(DC):
                    mm(bank[bq], W[:, kc, :], hT[:, kc, tts(tt)], kc == 0, kc == DC - 1, [Wb, hbuf[tt]], [pb[bq]], kc == DC - 1)
                S.op("act", lambda e, bq=bq, u=u: e.activation(tslot[u], bank[bq], AF.Square), writes=[pb[bq], tbuf[u]])
                mm(bank[bm], blk64, tslot[u], True, True, [tbuf[u], constbuf], [pb[bm]], True)
                S.op("act", lambda e, bm=bm, u=u: e.activation(tslot[2 + u], bank[bm], AF.Ln, bias=pcol(C_EPS)),
                     reads=[parambuf], writes=[pb[bm], tbuf[2 + u]])
                S.op("act", lambda e, u=u: e.activation(tslot[4 + u], tslot[2 + u], AF.Exp, scale=-0.5), reads=[tbuf[2 + u]], writes=[tbuf[4 + u]])
                S.op("dve", lambda e, bq=bq, u=u, tt=tt: e.scalar_tensor_tensor(dst[:, tts(tt)], bank[bq], pcol(gcol), tslot[4 + u], op0=ALU.mult, op1=ALU.mult),
                     reads=[tbuf[4 + u], parambuf], writes=[pb[bq], dstbuf[tt]])

        def project_v(W, Wb, nfeat):
            vt = vtm128 if nfeat == 128 else vtm64
            tpb = 512 // nfeat
            for g0 in range(0, NT, tpb):
                bv = 4 + un["v"] % 2
                un["v"] += 1
                for ti in range(tpb):
                    tile = g0 + ti
                    for kc in range(DC):
                        mm(bank[bv][:, ti * nfeat:(ti + 1) * nfeat], hT[:, kc, tile * 128:(tile + 1) * 128], W[:, kc, 0:nfeat],
                           kc == 0, kc == DC - 1, [Wb, hbuf[tile // 4]], [pb[bv]], kc == DC - 1 and ti == tpb - 1)
                S.op("act", lambda e, bv=bv, g0=g0: e.activation(vt[:, g0:g0 + tpb, :], bank[bv].rearrange("p (t f) -> p t f", t=tpb), AF.Copy),
                     writes=[pb[bv]] + sorted({vbuf[(g0 + ti) // 4] for ti in range(tpb)}, key=lambda b: b.name))

        def attn_pair(q, qbuf, kview, vfn, tiles, bias_off, sink_col, oc, sdouble=False):
            state = {}

            def qk_stage(i):
                jlo, jhi, segs = tiles(i)
                nk = jhi - jlo + 1
                def sbase(hh):
                    return (2 * hh + (i % 2)) * 512 if sdouble else hh * 1024

                def sbanks(hh):
                    if sdouble:
                        return [pb[2 * hh + (i % 2)]]
                    return [pb[2 * hh]] + ([pb[2 * hh + 1]] if nk > 4 else [])
                for kt in range(nk):
                    j = jlo + kt
                    for hh in range(2):
                        hs = slice(hh * 64, (hh + 1) * 64)
                        base = sbase(hh)
                        bk = (base + kt * 128) // 512
                        mm(ps[:, base + kt * 128: base + (kt + 1) * 128], kview[hs, j * 128:(j + 1) * 128], q[hs, i * 128:(i + 1) * 128],
                           True, True, [kbuf[j // 4], qbuf[i // 4]], [pb[bk]], kt == nk - 1 and hh == 1)
                for hh in range(2):
                    base = sbase(hh)
                    pbs = sbanks(hh)
                    for (c0, n, a0) in segs:
                        S.op("dve", lambda e, base=base, c0=c0, n=n, a0=a0, hh=hh: e.tensor_tensor(
                            ps[:, base + c0 * 128: base + (c0 + n) * 128], ps[:, base + c0 * 128: base + (c0 + n) * 128],
                            tb[:, bias_off(hh) + a0 * 128: bias_off(hh) + (a0 + n) * 128], op=ALU.add),
                            reads=[tbbuf], writes=pbs)
                    pi = un["w"] % 4
                    un["w"] += 1
                    S.op("act", lambda e, base=base, nk=nk, pi=pi: e.activation(pT[pi][:, 0:nk * 128], ps[:, base: base + nk * 128], AF.Exp),
                         writes=pbs + [pTbuf[pi]])
                    state[(i, hh)] = pi

            def pv_stage(i):
                jlo, jhi, segs = tiles(i)
                nk = jhi - jlo + 1
                bo = 5 + i % 3
                pis = [state.pop((i, hh)) for hh in range(2)]
                for kt in range(nk):
                    j = jlo + kt
                    for hh in range(2):
                        hs = slice(hh * 64, (hh + 1) * 64)
                        mm(bank[bo][hs, 0:128], vfn(j, hh), pT[pis[hh]][:, kt * 128:(kt + 1) * 128], kt == 0, kt == nk - 1,
                           [vbuf[j // 4], pTbuf[pis[hh]]], [pb[bo]], False)
                for kt in range(nk):
                    for hh in range(2):
                        hs = slice(hh * 64, (hh + 1) * 64)
                        mm(bank[bo][hs, 128:256], onesb[:, 0:64], pT[pis[hh]][:, kt * 128:(kt + 1) * 128], kt == 0, kt == nk - 1,
                           [onesbbuf, pTbuf[pis[hh]]], [pb[bo]], kt == nk - 1 and hh == 1)

            def norm_stage(i):
                bo = 5 + i % 3
                ri = i % 4
                if sink_col is not None:
                    S.op("act", lambda e, bo=bo, ri=ri: e.activation(rden[ri], bank[bo][:, 128:256], AF.Ln, bias=pcol(sink_col)),
                         reads=[parambuf], writes=[pb[bo], rdbuf[ri]])
                else:
                    S.op("act", lambda e, bo=bo, ri=ri: e.activation(rden[ri], bank[bo][:, 128:256], AF.Ln), writes=[pb[bo], rdbuf[ri]])
                S.op("act", lambda e, ri=ri: e.activation(rden[ri], rden[ri], AF.Exp, scale=-1.0), reads=[rdbuf[ri]], writes=[rdbuf[ri]])
                S.op("dve", lambda e, bo=bo, ri=ri: e.tensor_copy(pcop[ri], bank[bo][:, 0:128]), writes=[pb[bo], pcbuf[ri]])
                S.op("pool", lambda e, ri=ri, i=i: e.tensor_tensor(oT[oc][:, i * 128:(i + 1) * 128], pcop[ri], rden[ri], op=ALU.mult),
                     reads=[rdbuf[ri], pcbuf[ri]], writes=[obuf[oc][i // 4]])

            for i in range(NT + 2):
                if i < NT:
                    qk_stage(i)
                if 1 <= i <= NT:
                    pv_stage(i - 1)
                if i >= 2:
                    norm_stage(i - 2)

        def sw_tiles(n):
            lo, hi = max(n - 1, 0), min(n + 1, NT - 1)
            return lo, hi, [(0, hi - lo + 1, lo - n + 1)]

        tb_alias = [stagebuf[0], stagebuf[1], stagebuf[2]] + [mbuf[dc][tt] for dc in (4, 5, 6) for tt in range(TT)]

        def mixer(l):
            o = l * PL
            for p in range(4):
                Q, Qb, Qi = chunk()
                K, Kb, Ki = chunk()
                V, Vb, Vi = chunk()
                project_qk(Q, Qb, qA, qAbuf, o + C_NAQ8)
                release(Qi)
                project_qk(K, Kb, kk, kbuf, o + C_NAK)
                release(Ki)
                project_v(V, Vb, 128)
                release(Vi)
                if stop == (l, "proj"):
                    for tt in range(TT):
                        S.op("dve", lambda e, tt=tt: e.tensor_copy(xT[:, 0, tts(tt)], qA[:, tts(tt)]), reads=[qAbuf[tt]], writes=[xbuf[0][tt]])
                        S.op("dve", lambda e, tt=tt: e.tensor_copy(xT[:, 1, tts(tt)], kk[:, tts(tt)]), reads=[kbuf[tt]], writes=[xbuf[1][tt]])
                    return
                S.dma("sp", tbsem, lambda e, l=l, p=p: e.dma_start(out=tb[:, 0:2304], in_=nab_d[l, p]), writes=[tbbuf] + tb_alias)
                for hh in range(2):
                    S.op("dve", lambda e, hh=hh: e.tensor_tensor(tb[:, hh * 1152:(hh + 1) * 1152], tb[:, hh * 1152:(hh + 1) * 1152], nam, op=ALU.add),
                         reads=[tbbuf, maskbuf], writes=[tbbuf])
                attn_pair(qA, qAbuf, kk, lambda j, hh: vtm128[:, j, hh * 64:(hh + 1) * 64], na_tile_info,
                          lambda hh: hh * 1152, None, p)
            for g in range(2):
                Q0, Q0b, Q0i = chunk()
                Q1, Q1b, Q1i = chunk()
                K, Kb, Ki = chunk()
                V, Vb, Vi = chunk()
                project_qk(Q0, Q0b, qA, qAbuf, o + C_SWQ8)
                release(Q0i)
                project_qk(Q1, Q1b, qB, qBbuf, o + C_SWQ8)
                release(Q1i)
                project_qk(K, Kb, kk, kbuf, o + C_SWK)
                release(Ki)
                project_v(V, Vb, 64)
                release(Vi)
                S.dma("sp", tbsem, lambda e, g=g: e.dma_start(out=tb[:, 0:1536], in_=swb_d[g]), writes=[tbbuf] + tb_alias)
                for h4 in range(4):
                    S.op("dve", lambda e, h4=h4: e.tensor_tensor(tb[:, h4 * 384:(h4 + 1) * 384], tb[:, h4 * 384:(h4 + 1) * 384], swm, op=ALU.add),
                         reads=[tbbuf, maskbuf], writes=[tbbuf])
                for pp in range(2):
                    attn_pair(qA if pp == 0 else qB, qAbuf if pp == 0 else qBbuf, kk, lambda j, hh: vtm64[:, j, :], sw_tiles,
                              lambda hh, pp=pp: (2 * pp + hh) * 384, o + C_ESINK + 2 * g + pp, 4 + 2 * g + pp, sdouble=True)
            for dc in range(DC):
                GA, GAb, GAi = chunk()
                GB, GBb, GBi = chunk()
                BR, BRb, BRi = chunk()
                for tt in range(TT):
                    u = un["m"] % 2
                    un["m"] += 1
                    b0 = 4 * u
                    for kc in range(DC):
                        mm(bank[b0], GA[:, kc, :], hT[:, kc, tts(tt)], kc == 0, kc == DC - 1, [GAb, hbuf[tt]], [pb[b0]], kc == DC - 1)
                    for kc in range(DC):
                        mm(bank[b0 + 1], GB[:, kc, :], hT[:, kc, tts(tt)], kc == 0, kc == DC - 1, [GBb, hbuf[tt]], [pb[b0 + 1]], kc == DC - 1)
                    for c in range(4):
                        mm(bank[b0 + 2], BR[:, c, :], oT[c][:, tts(tt)], c == 0, c == 3, [BRb, obuf[c][tt]], [pb[b0 + 2]], c == 3)
                    for c in range(4):
                        mm(bank[b0 + 3], BR[:, 4 + c, :], oT[4 + c][:, tts(tt)], c == 0, c == 3, [BRb, obuf[4 + c][tt]], [pb[b0 + 3]], c == 3)
                    tA, tB = tslot[u], tslot[2 + u]
                    S.op("act", lambda e, b0=b0, tA=tA, dc=dc: e.activation(tA, bank[b0], AF.Sigmoid, bias=pcol(o + C_BG + dc)),
                         reads=[parambuf], writes=[pb[b0], tbuf[u]])
                    S.op("act", lambda e, b0=b0, tB=tB, dc=dc: e.activation(tB, bank[b0 + 1], AF.Sigmoid, bias=pcol(o + C_BG + 8 + dc)),
                         reads=[parambuf], writes=[pb[b0 + 1], tbuf[2 + u]])
                    S.op("dve", lambda e, b0=b0, tA=tA: e.tensor_tensor(tA, tA, bank[b0 + 2], op=ALU.mult), reads=[tbuf[u]], writes=[pb[b0 + 2], tbuf[u]])
                    S.op("dve", lambda e, b0=b0, tB=tB: e.tensor_tensor(tB, tB, bank[b0 + 3], op=ALU.mult), reads=[tbuf[2 + u]], writes=[pb[b0 + 3], tbuf[2 + u]])
                    S.op("dve", lambda e, tA=tA, tB=tB, dc=dc, tt=tt: e.tensor_tensor(merged[:, dc, tts(tt)], tA, tB, op=ALU.add),
                         reads=[tbuf[u], tbuf[2 + u]], writes=[mbuf[dc][tt]])
                release(GAi)
                release(GBi)
                release(BRi)
            for dcp in range(DC):
                WO, WOb, WOi = chunk()
                for tt in range(TT):
                    bw = un["d"] % 2 + 4
                    un["d"] += 1
                    for dc in range(DC):
                        mm(bank[bw], WO[:, dc, :], merged[:, dc, tts(tt)], dc == 0, dc == DC - 1, [WOb, mbuf[dc][tt]], [pb[bw]], dc == DC - 1)
                    S.op("dve", lambda e, bw=bw, dcp=dcp, tt=tt: e.tensor_tensor(xT[:, dcp, tts(tt)], bank[bw], xT[:, dcp, tts(tt)], op=ALU.add),
                         reads=[xbuf[dcp][tt]], writes=[pb[bw], xbuf[dcp][tt]])
                release(WOi)

        def run():
            if stop == (0, "load"):
                return
            for l in range(L):
                o = l * PL
                rmsnorm(o + C_FFN1)
                ffn()
                if stop == (l, "ffn1"):
                    return
                rmsnorm(o + C_MIX)
                if stop == (l, "norm"):
                    return
                mixer(l)
                if stop == (l, "proj"):
                    return
                if stop == (l, "mixer"):
                    return
                rmsnorm(o + C_FFN2)
                ffn()
                if stop == (l, "ffn2"):
                    return
        run()

        if dump == "h":
            for dc in range(DC):
                for tt in range(TT):
                    S.op("dve", lambda e, dc=dc, tt=tt: e.tensor_copy(xT[:, dc, tts(tt)], hT[:, dc, tts(tt)]), reads=[hbuf[tt]], writes=[xbuf[dc][tt]])
        if dump == "o":
            for c in range(8):
                for tt in range(TT):
                    S.op("dve", lambda e, c=c, tt=tt: e.tensor_copy(xT[:, c, tts(tt)], oT[c][:, tts(tt)]), reads=[obuf[c][tt]], writes=[xbuf[c][tt]])

        ybufs = []
        for i in range(NT):
            s = i % 4
            for half in range(2):
                bk = tu[0] % 2
                tu[0] += 1
                for q in range(4):
                    dc = half * 4 + q
                    S.op("pe", lambda e, i=i, dc=dc, bk=bk, q=q: e.transpose(bank[bk][:, q * 128:(q + 1) * 128], xT[:, dc, i * 128:(i + 1) * 128], ident),
                         reads=[xbuf[dc][i // 4], constbuf], writes=[pb[bk]], inc=(q == 3))
                S.op("dve", lambda e, s=s, half=half, bk=bk: e.tensor_copy(stage[s][:, half * 512:(half + 1) * 512], bank[bk]),
                     writes=[pb[bk], stagebuf[s]])
            yb = Buf("y%d" % i)
            ybufs.append(yb)
            S.dma("sp", ysem[s], lambda e, i=i, s=s: e.dma_start(out=y_d[i * 128:(i + 1) * 128, :], in_=stage[s]), reads=[stagebuf[s]], writes=[yb])
        S.final_wait("sp", ybufs)
        S.emit(st)
    return nc


_HOST_CACHE = {}


def prep_shared(inputs):
    inp = {k: np.asarray(v, dtype=np.float32) for k, v in inputs.items() if k != "x"}
    ws = build_wstream(inp)
    params = build_params(inp)
    consts = build_consts()
    swb, swm, nab, nam = build_bias_tables(inp)
    return {"wstream": ws, "params": params, "consts": consts, "swb": swb, "swm": swm, "nab": nab, "nam": nam}


def kernel(**inputs):
    x = np.ascontiguousarray(np.asarray(inputs["x"], dtype=np.float32))
    shared = prep_shared(inputs)
    nc = build_nc()
    in_maps = [dict(shared, x=x[b]) for b in range(NCORES)]
    res = run_bass_kernel_spmd(nc, in_maps, core_ids=list(range(NCORES)))
    return np.stack([np.asarray(r["y"], dtype=np.float32) for r in res.results], axis=0)
```
